# Optimizing a Trainium2 kernel written in Bass

```python
import math
import jax, jax.numpy as jnp
from jax import lax
import numpy as np

D_MODEL = 1024
BATCH = 1
SEQ = 16384
DEPTH = 4

MEM_LEN = 256
N_MIXERS = 3
EXPAND = 2
BRANCH = EXPAND * D_MODEL
CONV_WIDTH = 3
DIFF_HEAD_DIM = 64
DIFF_HEADS = BRANCH // (2 * DIFF_HEAD_DIM)
SWA_HEAD_DIM = 64
SWA_Q_HEADS = BRANCH // SWA_HEAD_DIM
SWA_KV_HEADS = SWA_Q_HEADS // 8
WINDOW = 128
Q_BLOCK = 128
MEM_HEADS = 4
MEM_HEAD_DIM = 64
MEM_WIDTH = MEM_HEADS * MEM_HEAD_DIM
GATE_WIDTH = BRANCH + MEM_WIDTH
ALIBI_MAX_BIAS = 8.0
EPS = 1e-6
NEG_INF = -1e30

CONV_COLS = (BRANCH, BRANCH, BRANCH, MEM_WIDTH, GATE_WIDTH)
DIFF_COLS = (DIFF_HEADS * 2 * DIFF_HEAD_DIM, DIFF_HEADS * 2 * DIFF_HEAD_DIM,
             DIFF_HEADS * 2 * DIFF_HEAD_DIM, MEM_WIDTH, GATE_WIDTH)
SWA_COLS = (SWA_Q_HEADS * SWA_HEAD_DIM, SWA_KV_HEADS * SWA_HEAD_DIM,
            SWA_KV_HEADS * SWA_HEAD_DIM, MEM_WIDTH, GATE_WIDTH)

kernel_name = "interleaved_conv_diffattn_swa_sink_hybrid"


def rms_norm(x, g):
    xf = x.astype(jnp.float32)
    y = xf * lax.rsqrt(jnp.mean(xf * xf, axis=-1, keepdims=True) + EPS)
    return (y * g.astype(jnp.float32)).astype(x.dtype)


def split_cols(t, widths):
    idx = [int(v) for v in np.cumsum(widths)[:-1]]
    return jnp.split(t, idx, axis=-1)


def alibi_slopes(n_heads):
    return 2.0 ** (-ALIBI_MAX_BIAS * jnp.arange(1, n_heads + 1, dtype=jnp.float32) / n_heads)


def diff_lambda_init(layer_idx):
    return 0.8 - 0.6 * math.exp(-0.3 * layer_idx)


def short_conv_mixer(bg, cg, u, conv_w):
    s = u.shape[1]
    z = cg * u
    zp = jnp.pad(z, ((0, 0), (CONV_WIDTH - 1, 0), (0, 0)))
    conv = sum(conv_w[j] * zp[:, j:j + s] for j in range(CONV_WIDTH))
    return bg * conv


def diff_attention(q, k, v, positions, lam, slopes):
    b, s = q.shape[:2]
    nb = s // Q_BLOCK
    scale = DIFF_HEAD_DIM ** -0.5
    kf = k.astype(jnp.float32)
    vf = v.astype(jnp.float32)
    qb = jnp.moveaxis(q.reshape(b, nb, Q_BLOCK, DIFF_HEADS, 2, DIFF_HEAD_DIM), 1, 0)
    pb = jnp.moveaxis(positions.reshape(b, nb, Q_BLOCK), 1, 0)

    def block(args):
        q_blk, p_blk = args
        logits = jnp.einsum('bqhcd,bshcd->bchqs', q_blk.astype(jnp.float32), kf) * scale
        rel = p_blk[:, :, None] - positions[:, None, :]
        alibi = -slopes[None, :, None, None] * rel[:, None].astype(jnp.float32)
        logits = jnp.where((rel >= 0)[:, None, None], logits + alibi[:, None], NEG_INF)
        probs = jax.nn.softmax(logits, axis=-1)
        weights = probs[:, 0] - lam * probs[:, 1]
        return jnp.einsum('bhqs,bshe->bqhe', weights, vf)

    out = lax.map(block, (qb, pb))
    return jnp.moveaxis(out, 0, 1).reshape(b, s, DIFF_HEADS, 2 * DIFF_HEAD_DIM)


def swa_with_sinks(q, k, v, positions, sinks, slopes):
    b, s = q.shape[:2]
    nb = s // WINDOW
    g = SWA_Q_HEADS // SWA_KV_HEADS
    scale = SWA_HEAD_DIM ** -0.5
    qb = q.reshape(b, nb, WINDOW, SWA_KV_HEADS, g, SWA_HEAD_DIM).astype(jnp.float32)

    def band(t):
        tb = t.reshape(b, nb, WINDOW, SWA_KV_HEADS, SWA_HEAD_DIM).astype(jnp.float32)
        prev = jnp.pad(tb[:, :-1], ((0, 0), (1, 0), (0, 0), (0, 0), (0, 0)))
        return jnp.concatenate([prev, tb], axis=2)

    kb, vb = band(k), band(v)
    pos_b = positions.reshape(b, nb, WINDOW)
    pos_prev = jnp.pad(pos_b[:, :-1], ((0, 0), (1, 0), (0, 0)))
    pos_band = jnp.concatenate([pos_prev, pos_b], axis=2)
    key_valid = jnp.concatenate(
        [jnp.broadcast_to((jnp.arange(nb) > 0)[:, None], (nb, WINDOW)),
         jnp.ones((nb, WINDOW), dtype=bool)], axis=1)
    rel = pos_b[..., :, None] - pos_band[..., None, :]
    mask = key_valid[None, :, None, :] & (rel >= 0) & (rel < WINDOW)
    logits = jnp.einsum('bnqkgd,bnskd->bnkgqs', qb, kb) * scale
    logits = logits - slopes.reshape(SWA_KV_HEADS, g)[None, None, :, :, None, None] \
        * rel[:, :, None, None].astype(jnp.float32)
    logits = jnp.where(mask[:, :, None, None], logits, NEG_INF)
    sink = sinks.astype(jnp.float32).reshape(SWA_KV_HEADS, g)[None, None, :, :, None, None]
    m = jnp.maximum(jnp.max(logits, axis=-1, keepdims=True), sink)
    e = jnp.exp(logits - m)
    probs = e / (jnp.sum(e, axis=-1, keepdims=True) + jnp.exp(sink - m))
    out = jnp.einsum('bnkgqs,bnskd->bnqkgd', probs, vb)
    return out.reshape(b, s, SWA_Q_HEADS * SWA_HEAD_DIM)


def memory_attention(q, mem_n, w_mem_kv):
    b, s = q.shape[:2]
    kv = jnp.einsum('bmd,de->bme', mem_n, w_mem_kv)
    km, vm = jnp.split(kv, 2, axis=-1)
    km = km.reshape(b, -1, MEM_HEADS, MEM_HEAD_DIM).astype(jnp.float32)
    vm = vm.reshape(b, -1, MEM_HEADS, MEM_HEAD_DIM).astype(jnp.float32)
    qh = q.reshape(b, s, MEM_HEADS, MEM_HEAD_DIM).astype(jnp.float32)
    logits = jnp.einsum('bshd,bmhd->bhsm', qh, km) * (MEM_HEAD_DIM ** -0.5)
    probs = jax.nn.softmax(logits, axis=-1)
    return jnp.einsum('bhsm,bmhd->bshd', probs, vm).reshape(b, s, MEM_WIDTH)


def setup_inputs(seed: int = 0) -> dict:
    key = jax.random.key(seed)
    ks = iter(jax.random.split(key, 64))

    def nrm(shape, scale):
        return scale * jax.random.normal(next(ks), shape, jnp.float32)

    def gain(n):
        return 1.0 + nrm((n,), 0.02)

    inp = {}
    inp["x"] = nrm((BATCH, SEQ, D_MODEL), 1.0)
    inp["mem"] = nrm((BATCH, MEM_LEN, D_MODEL), 1.0)
    inp["positions"] = jnp.broadcast_to(jnp.arange(SEQ, dtype=jnp.int32)[None], (BATCH, SEQ))
    for i in range(DEPTH):
        kind = i % N_MIXERS
        cols = (CONV_COLS, DIFF_COLS, SWA_COLS)[kind]
        inp[f"norm_pre_{i}"] = gain(D_MODEL)
        inp[f"norm_post_{i}"] = gain(D_MODEL)
        inp[f"norm_mem_{i}"] = gain(D_MODEL)
        inp[f"w_in_{i}"] = nrm((D_MODEL, sum(cols)), D_MODEL ** -0.5)
        inp[f"w_mem_kv_{i}"] = nrm((D_MODEL, 2 * MEM_WIDTH), D_MODEL ** -0.5)
        if kind == 0:
            inp[f"conv_w_{i}"] = nrm((CONV_WIDTH, BRANCH), CONV_WIDTH ** -0.5)
        elif kind == 1:
            inp[f"lambda_q1_{i}"] = nrm((DIFF_HEAD_DIM,), 0.1)
            inp[f"lambda_k1_{i}"] = nrm((DIFF_HEAD_DIM,), 0.1)
            inp[f"lambda_q2_{i}"] = nrm((DIFF_HEAD_DIM,), 0.1)
            inp[f"lambda_k2_{i}"] = nrm((DIFF_HEAD_DIM,), 0.1)
            inp[f"subln_{i}"] = gain(2 * DIFF_HEAD_DIM)
        else:
            inp[f"sinks_{i}"] = nrm((SWA_Q_HEADS,), 0.5)
        inp[f"w_out_{i}"] = nrm((GATE_WIDTH, D_MODEL), GATE_WIDTH ** -0.5)
    return inp


def reference(x, mem, positions,
              norm_pre_0, norm_post_0, norm_mem_0, w_in_0, w_mem_kv_0, conv_w_0, w_out_0,
              norm_pre_1, norm_post_1, norm_mem_1, w_in_1, w_mem_kv_1,
              lambda_q1_1, lambda_k1_1, lambda_q2_1, lambda_k2_1, subln_1, w_out_1,
              norm_pre_2, norm_post_2, norm_mem_2, w_in_2, w_mem_kv_2, sinks_2, w_out_2,
              norm_pre_3, norm_post_3, norm_mem_3, w_in_3, w_mem_kv_3, conv_w_3, w_out_3):
    layers = [
        dict(pre=norm_pre_0, post=norm_post_0, mem_norm=norm_mem_0, w_in=w_in_0,
             w_mem_kv=w_mem_kv_0, w_out=w_out_0, conv_w=conv_w_0),
        dict(pre=norm_pre_1, post=norm_post_1, mem_norm=norm_mem_1, w_in=w_in_1,
             w_mem_kv=w_mem_kv_1, w_out=w_out_1, lq1=lambda_q1_1, lk1=lambda_k1_1,
             lq2=lambda_q2_1, lk2=lambda_k2_1, subln=subln_1),
        dict(pre=norm_pre_2, post=norm_post_2, mem_norm=norm_mem_2, w_in=w_in_2,
             w_mem_kv=w_mem_kv_2, w_out=w_out_2, sinks=sinks_2),
        dict(pre=norm_pre_3, post=norm_post_3, mem_norm=norm_mem_3, w_in=w_in_3,
             w_mem_kv=w_mem_kv_3, w_out=w_out_3, conv_w=conv_w_3),
    ]
    b, s = x.shape[:2]
    diff_slopes = alibi_slopes(DIFF_HEADS)
    swa_slopes = alibi_slopes(SWA_Q_HEADS)
    for i in range(DEPTH):
        p = layers[i]
        kind = i % N_MIXERS
        h = rms_norm(x, p["pre"])
        proj = jnp.einsum('bsd,de->bse', h, p["w_in"])
        mem_n = rms_norm(mem, p["mem_norm"])
        if kind == 0:
            bg, cg, u, q_mem, gate = split_cols(proj, CONV_COLS)
            mix = short_conv_mixer(bg, cg, u, p["conv_w"])
        elif kind == 1:
            q, k, v, q_mem, gate = split_cols(proj, DIFF_COLS)
            q = q.reshape(b, s, DIFF_HEADS, 2, DIFF_HEAD_DIM)
            k = k.reshape(b, s, DIFF_HEADS, 2, DIFF_HEAD_DIM)
            v = v.reshape(b, s, DIFF_HEADS, 2 * DIFF_HEAD_DIM)
            lam_init = diff_lambda_init(i)
            lam = (jnp.exp(jnp.sum(p["lq1"].astype(jnp.float32) * p["lk1"].astype(jnp.float32)))
                   - jnp.exp(jnp.sum(p["lq2"].astype(jnp.float32) * p["lk2"].astype(jnp.float32)))
                   + lam_init)
            o = diff_attention(q, k, v, positions, lam, diff_slopes)
            mix = (rms_norm(o, p["subln"]) * (1.0 - lam_init)).reshape(b, s, BRANCH)
        else:
            q, k, v, q_mem, gate = split_cols(proj, SWA_COLS)
            mix = swa_with_sinks(q, k, v, positions, p["sinks"], swa_slopes)
        mem_out = memory_attention(q_mem, mem_n, p["w_mem_kv"])
        y = jnp.concatenate([mix.astype(x.dtype), mem_out.astype(x.dtype)], axis=-1) * jax.nn.silu(gate)
        y = jnp.einsum('bse,ed->bsd', y, p["w_out"])
        x = x + rms_norm(y, p["post"])
    return x
```

```python
import contextlib
import math
import numpy as np
import ml_dtypes
import concourse.bass as bass
import concourse.mybir as mybir
from concourse.bass_utils import run_bass_kernel_spmd

F32 = mybir.dt.float32
BF16 = mybir.dt.bfloat16
AF = mybir.ActivationFunctionType
ALU = mybir.AluOpType

NCORES = 8
D = 1024
KC = 8
BR = 2048
GW = 2304
GC = 18
EPS = 1e-6
NEG = -1e30
SAME_ENGINE_SYNC = True


class _Op:
    __slots__ = ("eng", "fn", "reads", "writes", "stream", "deps", "signal", "sig_sem", "sig_val")

    def __init__(self, eng, fn, reads, writes, stream):
        self.eng = eng
        self.fn = fn
        self.reads = tuple(reads)
        self.writes = tuple(writes)
        self.stream = stream
        self.deps = ()
        self.signal = False
        self.sig_sem = None
        self.sig_val = 0


class Prog:
    ENGS = ("pe", "act", "dve", "pool", "sp")

    def __init__(self, nc):
        self.nc = nc
        self.ops = []
        self.stack = contextlib.ExitStack()

    def sbuf(self, name, shape, dtype):
        return self.stack.enter_context(self.nc.sbuf_tensor(name, list(shape), dtype))

    def psum(self, name, shape, dtype):
        return self.stack.enter_context(self.nc.psum_tensor(name, list(shape), dtype))

    def op(self, eng, fn, reads=(), writes=(), stream=None):
        o = _Op(eng, fn, reads, writes, stream)
        self.ops.append(o)
        return o

    def dma(self, q, out, in_, reads, writes, stream):
        return self.op(q, lambda e: e.dma_start(out=out, in_=in_), reads, writes, stream)

    def emit(self):
        nc = self.nc
        ops = self.ops
        last_writer = {}
        readers = {}
        for i, o in enumerate(ops):
            deps = set()
            for b in o.reads:
                w = last_writer.get(b)
                if w is not None:
                    deps.add(w)
            for b in o.writes:
                w = last_writer.get(b)
                if w is not None:
                    deps.add(w)
                r = readers.get(b)
                if r:
                    deps.update(r)
            deps.discard(i)
            o.deps = deps
            for b in o.reads:
                readers.setdefault(b, []).append(i)
            for b in o.writes:
                last_writer[b] = i
                readers[b] = []
        for o in ops:
            for d in o.deps:
                od = ops[d]
                if od.stream is not None or od.eng != o.eng or (SAME_ENGINE_SYNC and od.eng != "pe"):
                    od.signal = True
        sems = {}
        counts = {}
        for o in ops:
            if not o.signal:
                continue
            key = ("d", o.stream) if o.stream is not None else ("e", o.eng)
            if key not in sems:
                sems[key] = self.stack.enter_context(nc.semaphore("s_%s_%s" % key))
                counts[key] = 0
            counts[key] += 16 if o.stream is not None else 1
            o.sig_sem = key
            o.sig_val = counts[key]
        self.sem_counts = dict(counts)
        per_eng = {e: [o for o in ops if o.eng == e] for e in self.ENGS}
        waited = {}

        def run(engname, e):
            for o in per_eng[engname]:
                need = {}
                for d in o.deps:
                    od = ops[d]
                    if not od.signal:
                        continue
                    if od.stream is None and od.eng == engname and not (SAME_ENGINE_SYNC and engname != "pe"):
                        continue
                    k = od.sig_sem
                    if od.sig_val > need.get(k, 0):
                        need[k] = od.sig_val
                for k, v in need.items():
                    if waited.get((engname, k), 0) >= v:
                        continue
                    waited[(engname, k)] = v
                    e.wait_ge(sems[k], v)
                ins = o.fn(e)
                if o.signal:
                    ins.then_inc(sems[o.sig_sem], 16 if o.stream is not None else 1)

        with nc.Block() as block:
            @block.tensor
            def _(e):
                run("pe", e)

            @block.scalar
            def _(e):
                run("act", e)

            @block.vector
            def _(e):
                run("dve", e)

            @block.gpsimd
            def _(e):
                run("pool", e)

            @block.sync
            def _(e):
                run("sp", e)

    def close(self):
        self.stack.close()


def MM(out, lhsT, rhs, start, stop):
    return lambda e: e.matmul(out, lhsT, rhs, start=start, stop=stop)


def ACT(out, in_, func, bias=None, scale=None):
    kw = {}
    if bias is not None:
        kw["bias"] = bias
    if scale is not None:
        kw["scale"] = scale
    return lambda e: e.activation(out, in_, func, **kw)


def TT(out, a, b, op):
    return lambda e: e.tensor_tensor(out, a, b, op)


def TS(out, a, s1, op0, s2=None, op1=None):
    if op1 is None:
        return lambda e: e.tensor_scalar(out, a, s1, None, op0)
    return lambda e: e.tensor_scalar(out, a, s1, s2, op0, op1)


def STT(out, in0, scalar, in1, op0, op1):
    return lambda e: e.scalar_tensor_tensor(out, in0, scalar, in1, op0, op1)


def CP(out, in_):
    return lambda e: e.tensor_copy(out, in_)


def MSET(out, v):
    return lambda e: e.memset(out, v)


def RECIP(out, in_):
    return lambda e: e.reciprocal(out, in_)


def alibi_slopes(n):
    return (2.0 ** (-8.0 * np.arange(1, n + 1, dtype=np.float64) / n)).astype(np.float32)


def bf(x):
    return np.ascontiguousarray(np.asarray(x).astype(ml_dtypes.bfloat16))


def arrange_blocks(W, colsets):
    nb = len(colsets)
    out = np.zeros((nb, 128, KC, 512), np.float32)
    Wr = W.reshape(KC, 128, -1)
    for b, cs in enumerate(colsets):
        cs = np.asarray(cs)
        ok = cs >= 0
        blk = np.zeros((KC, 128, 512), np.float32)
        blk[:, :, ok] = Wr[:, :, cs[ok]]
        out[b] = blk.transpose(1, 0, 2)
    return out


def arrange_wout(Wo):
    return np.ascontiguousarray(Wo.reshape(GC, 128, D).transpose(1, 0, 2))


def arrange_vec(v, n):
    return np.ascontiguousarray(np.asarray(v, np.float32).reshape(n, 128).T)


def tail_cols(q0, g0):
    return list(range(q0, q0 + 256)) + list(range(g0 + 2048, g0 + 2304))


def conv_colsets():
    cs = []
    for fc in range(16):
        c = []
        for base in (0, 2048, 4096, 6400):
            c += list(range(base + fc * 128, base + fc * 128 + 128))
        cs.append(c)
    cs.append(tail_cols(6144, 6400))
    return cs


def swa_colsets():
    cs = []
    for kb in range(2):
        c = []
        for kvh in (2 * kb, 2 * kb + 1):
            for par in range(2):
                blk = [-1] * 128
                for d in range(64):
                    blk[par * 64 + d] = 2048 + kvh * 64 + d
                c += blk
        cs.append(c)
    cs.append(list(range(2304, 2560)) + [-1] * 256)
    for b in range(8):
        c = []
        for i in (2 * b, 2 * b + 1):
            c += list(range(i * 128, i * 128 + 128))
            c += list(range(2816 + i * 128, 2816 + i * 128 + 128))
        cs.append(c)
    cs.append(tail_cols(2560, 2816))
    return cs


def diffpost_colsets():
    return [tail_cols(6144, 6400)]


def memkv_colsets():
    cs = []
    c = []
    for hm in range(4):
        blk = [-1] * 128
        for d in range(64):
            blk[(hm % 2) * 64 + d] = hm * 64 + d
        c += blk
    cs.append(c)
    cs.append(list(range(256, 512)) + [-1] * 256)
    return cs


def build_tok(kind, TC, TP, emit_h_next=False):
    HALO = {"conv": 16, "swa": 128, "diffpost": 0}[kind]
    NP = TC // TP
    NG = TP // 512
    nblk = {"conv": 17, "swa": 12, "diffpost": 1}[kind]
    nc = bass.Bass("TRN2", target_bir_lowering=False)
    dt = nc.dram_tensor
    xT = dt("xT", [D, HALO + TC], F32, kind="ExternalInput").ap()
    memT = dt("memT", [D, 256], F32, kind="ExternalInput").ap()
    wblk = dt("wblk", [nblk, 128, KC, 512], F32, kind="ExternalInput").ap()
    wkv = dt("wkv", [2, 128, KC, 512], F32, kind="ExternalInput").ap()
    wo = dt("wo", [128, GC, D], F32, kind="ExternalInput").ap()
    gpre_d = dt("gpre", [128, KC], F32, kind="ExternalInput").ap()
    gpost_d = dt("gpost", [128, KC], F32, kind="ExternalInput").ap()
    gmem_d = dt("gmem", [128, KC], F32, kind="ExternalInput").ap()
    if emit_h_next:
        gnext_d = dt("gnext", [128, KC], F32, kind="ExternalInput").ap()
        hnT = dt("hnT", [D, TC], BF16, kind="ExternalOutput").ap()
    if kind == "conv":
        cw_d = dt("cw", [128, 16, 3], F32, kind="ExternalInput").ap()
    if kind == "swa":
        rel_d = dt("rel2", [128, 256], F32, kind="ExternalInput").ap()
        mneg_d = dt("mneg2", [128, 256], F32, kind="ExternalInput").ap()
        hv_d = dt("hvneg", [128, 1], F32, kind="ExternalInput").ap()
        sk_d = dt("sinks", [128, 16], F32, kind="ExternalInput").ap()
    if kind == "diffpost":
        mixT = dt("mixT", [BR, TC], BF16, kind="ExternalInput").ap()
    xoT = dt("xoT", [D, TC], F32, kind="ExternalOutput").ap()

    xT_v = xT.rearrange("(k p) t -> p k t", p=128)
    xoT_v = xoT.rearrange("(k p) t -> p k t", p=128)
    memT_v = memT.rearrange("(k p) t -> p k t", p=128)

    p = Prog(nc)
    NT = HALO + TP
    hT = p.sbuf("hT", [128, KC, NT], BF16)
    yg = p.sbuf("yg", [128, GC, TP], BF16)
    wo_sb = p.sbuf("wo_sb", [128, GC, D], BF16)
    xg = p.sbuf("xg", [128, KC, 512], F32)
    yT = p.sbuf("yT", [128, KC, 512], F32)
    NW = 3
    wb = [p.sbuf("wb%d" % i, [128, KC, 512], BF16) for i in range(NW)]
    sq = [p.sbuf("sq%d" % i, [128, 512], BF16) for i in range(2)]
    rr = p.sbuf("rr", [128, 512], F32)
    tmp = [p.sbuf("tmp%d" % i, [128, 512], F32) for i in range(2)]
    ones_b = p.sbuf("ones_b", [128, 128], BF16)
    onesp = [p.sbuf("onesp%d" % i, [128, 128], BF16) for i in range(2)]
    gpre = p.sbuf("gpre_s", [128, KC], F32)
    gpost = p.sbuf("gpost_s", [128, KC], F32)
    gmem = p.sbuf("gmem_s", [128, KC], F32)
    mnT = p.sbuf("mnT", [128, KC, 256], BF16)
    KmT = p.sbuf("KmT", [128, 4, 256], BF16)
    Vmp = p.sbuf("Vmp", [128, 4, 2, 128], BF16)
    qmT = p.sbuf("qmT", [128, 2, TP], BF16)
    sgt = p.sbuf("sgt", [128, 2, TP], BF16)
    pT = [p.sbuf("pT%d" % i, [128, 512], BF16) for i in range(2)]
    if emit_h_next:
        gnext = p.sbuf("gnext_s", [128, KC], F32)
        hn = p.sbuf("hn", [128, KC, 512], BF16)
    if kind == "conv":
        cw = p.sbuf("cw_s", [128, 16, 3], F32)
        zb = p.sbuf("zb", [128, HALO + TP], F32)
        u_sb = p.sbuf("u_sb", [128, 512], F32)
        sg = p.sbuf("sg", [128, 512], F32)
        cb = p.sbuf("cb", [128, 512], F32)
    if kind == "swa":
        rel2 = p.sbuf("rel2_s", [128, 256], F32)
        mneg2 = p.sbuf("mneg2_s", [128, 256], F32)
        hvneg = p.sbuf("hv_s", [128, 1], F32)
        sinks = p.sbuf("sinks_s", [128, 16], F32)
        esink = p.sbuf("esink", [128, 16], F32)
        NTL = NT // 128
        KTp = p.sbuf("KTp", [128, 8, NT], BF16)
        Vp = p.sbuf("Vp", [128, 8, NTL, 128], BF16)
        QT = p.sbuf("QT", [128, TP], BF16)
        sgq = p.sbuf("sgq", [128, TP], F32)
        b4 = p.sbuf("b4", [128, 512], F32)
        b4f = p.sbuf("b4f", [128, 512], F32)
        sb4 = p.sbuf("sb4", [128, 512], F32)
        rden = p.sbuf("rden", [128, 512], F32)
    ps = [p.psum("ps%d" % i, [128, 512], F32) for i in range(8)]

    p.op("dve", MSET(ones_b[:], 1.0), [], ["ones_b"])
    for par in range(2):
        p.op("dve", MSET(onesp[par][:], 0.0), [], [("onesp", par)])
        p.op("dve", MSET(onesp[par][:, par * 64:(par + 1) * 64], 1.0), [], [("onesp", par)])
    p.dma("sp", gpre[:], gpre_d, [], ["gpre"], "c0")
    p.dma("sp", gpost[:], gpost_d, [], ["gpost"], "c1")
    p.dma("sp", gmem[:], gmem_d, [], ["gmem"], "c2")
    if emit_h_next:
        p.dma("sp", gnext[:], gnext_d, [], ["gnext"], "c3")
    if kind == "conv":
        p.dma("sp", cw[:], cw_d, [], ["cw"], "c4")
    if kind == "swa":
        p.dma("sp", rel2[:], rel_d, [], ["rel2"], "c4")
        p.dma("sp", mneg2[:], mneg_d, [], ["mneg2"], "c5")
        p.dma("sp", hvneg[:], hv_d, [], ["hvneg"], "c6")
        p.dma("sp", sinks[:], sk_d, [], ["sinks"], "c7")
        p.op("act", ACT(esink[:], sinks[:], AF.Exp), ["sinks"], ["esink"])
        p.op("pool", MSET(Vp[:], 0.0), [], ["Vp"])
    p.op("pool", MSET(Vmp[:], 0.0), [], ["Vmp"])

    wstate = {"n": 0}

    def load_w(src_ap):
        s = wstate["n"] % NW
        wstate["n"] += 1
        p.dma("pool", wb[s][:], src_ap, [], [("wb", s)], "w%d" % s)
        return s

    sqi = {"n": 0}

    def norm_stats(src_of_k, N, nparts_scale, bank, tagreads):
        for k in range(KC):
            s = sqi["n"] % 2
            sqi["n"] += 1
            src = src_of_k(k)
            p.op("pool", TT(sq[s][:, :N], src, src, ALU.mult), tagreads, [("sq", s)])
            p.op("pe", MM(ps[bank][:, :N], ones_b[:], sq[s][:, :N], k == 0, k == KC - 1),
                 ["ones_b", ("sq", s)], [("ps", bank)])
        p.op("act", ACT(rr[:, :N], ps[bank][:, :N], AF.Ln, bias=EPS, scale=nparts_scale), [("ps", bank)], ["rr"])
        p.op("act", ACT(rr[:, :N], rr[:, :N], AF.Exp, scale=-0.5), ["rr"], ["rr"])

    def proj(bank, slot, c0, M, t0, N, pkey=None):
        for k in range(KC):
            p.op("pe", MM(ps[bank][:M, :N], wb[slot][:, k, c0:c0 + M], hT[:, k, t0:t0 + N], k == 0, k == KC - 1),
                 [("wb", slot), "hT"], [("ps", bank)])

    p.dma("sp", xg[:, :, 0:256], memT_v, [], ["xg"], "xin")
    norm_stats(lambda k: xg[:, k, 0:256], 256, 1.0 / D, 0, ["xg"])
    for k in range(KC):
        p.op("dve", STT(mnT[:, k, :], xg[:, k, 0:256], gmem[:, k:k + 1], rr[:, 0:256], ALU.mult, ALU.mult),
             ["xg", "gmem", "rr"], ["mnT"])
    sk = load_w(wkv[0])
    sv = load_w(wkv[1])
    for hm in range(4):
        bank = 1 + (hm % 2)
        for k in range(KC):
            p.op("pe", MM(ps[bank][:, :256], wb[sk][:, k, hm * 128:(hm + 1) * 128], mnT[:, k, :], k == 0, k == KC - 1),
                 [("wb", sk), "mnT"], [("ps", bank)])
        p.op("act", ACT(KmT[:, hm, :], ps[bank][:, :256], AF.Copy, scale=0.125), [("ps", bank)], ["KmT"])
    for t in range(2):
        bank = 3 + t
        for k in range(KC):
            p.op("pe", MM(ps[bank][:, :256], mnT[:, k, t * 128:(t + 1) * 128], wb[sv][:, k, 0:256], k == 0, k == KC - 1),
                 [("wb", sv), "mnT"], [("ps", bank)])
        for hm in range(4):
            par = hm % 2
            p.op("dve", CP(Vmp[:, hm, t, par * 64:(par + 1) * 64], ps[bank][:, hm * 64:(hm + 1) * 64]),
                 [("ps", bank)], ["Vmp"])

    for pp in range(NP):
        tb = pp * TP
        if pp == 0:
            for c3 in range(3):
                p.dma("pool", wo_sb[:, c3 * 6:(c3 + 1) * 6, :], wo[:, c3 * 6:(c3 + 1) * 6, :], [], ["wo_sb"], "wo")
        groups = []
        if HALO:
            groups.append((0, HALO))
        for g in range(NG):
            groups.append((HALO + g * 512, 512))
        for (t0, N) in groups:
            p.dma("sp", xg[:, :, 0:N], xT_v[:, :, tb + t0:tb + t0 + N], [], ["xg"], "xin")
            norm_stats(lambda k, N=N: xg[:, k, 0:N], N, 1.0 / D, 0, ["xg"])
            for k in range(KC):
                p.op("dve", STT(hT[:, k, t0:t0 + N], xg[:, k, 0:N], gpre[:, k:k + 1], rr[:, 0:N], ALU.mult, ALU.mult),
                     ["xg", "gpre", "rr"], ["hT"])

        if kind == "conv":
            for fc in range(16):
                s = load_w(wblk[fc])
                for (sec, bank) in ((1, 0), (2, 1)):
                    proj(bank, s, sec * 128, 128, 0, HALO)
                p.op("act", ACT(u_sb[:, :HALO], ps[1][:, :HALO], AF.Copy), [("ps", 1)], ["u_sb"])
                p.op("dve", TT(zb[:, 0:HALO], ps[0][:, :HALO], u_sb[:, :HALO], ALU.mult), [("ps", 0), "u_sb"], [("zb", -1)])
                for g in range(NG):
                    t0 = HALO + g * 512
                    bb = 4 * (g % 2)
                    for sec in range(4):
                        proj(bb + sec, s, sec * 128, 128, t0, 512)
                    p.op("act", ACT(u_sb[:], ps[bb + 2][:], AF.Copy), [("ps", bb + 2)], ["u_sb"])
                    p.op("act", ACT(sg[:], ps[bb + 3][:], AF.Silu), [("ps", bb + 3)], ["sg"])
                    p.op("dve", TT(zb[:, t0:t0 + 512], ps[bb + 1][:], u_sb[:], ALU.mult), [("ps", bb + 1), "u_sb"], [("zb", g)])
                    zr = [("zb", g - 1), ("zb", g)]
                    p.op("pool", TS(cb[:], zb[:, t0 - 2:t0 + 510], cw[:, fc, 0:1], ALU.mult), zr + ["cw"], ["cb"])
                    p.op("dve", STT(cb[:], zb[:, t0 - 1:t0 + 511], cw[:, fc, 1:2], cb[:], ALU.mult, ALU.add), zr + ["cw", "cb"], ["cb"])
                    p.op("dve", STT(cb[:], zb[:, t0:t0 + 512], cw[:, fc, 2:3], cb[:], ALU.mult, ALU.add), zr + ["cw", "cb"], ["cb"])
                    p.op("dve", TT(tmp[0][:], ps[bb][:], cb[:], ALU.mult), [("ps", bb), "cb"], [("tmp", 0)])
                    p.op("dve", TT(yg[:, fc, g * 512:(g + 1) * 512], tmp[0][:], sg[:], ALU.mult), [("tmp", 0), "sg"], [("yg", fc)])
        elif kind == "diffpost":
            mix_v = mixT.rearrange("(c p) t -> p c t", p=128)
            for c4 in range(4):
                p.dma("sp", yg[:, c4 * 4:(c4 + 1) * 4, :], mix_v[:, c4 * 4:(c4 + 1) * 4, tb:tb + TP], [],
                      [("yg", c4 * 4 + i) for i in range(4)], "mixin%d" % c4)
        elif kind == "swa":
            slopes = alibi_slopes(32)
            for kb in range(2):
                s = load_w(wblk[kb])
                for q in range(4):
                    idx = kb * 4 + q
                    for (t0, N) in groups:
                        bank = q % 4
                        proj(bank, s, q * 128, 128, t0, N)
                        p.op("act", ACT(KTp[:, idx, t0:t0 + N], ps[bank][:, :N], AF.Copy, scale=0.125), [("ps", bank)], ["KTp"])
            s = load_w(wblk[2])
            for tl in range(NTL):
                bank = 4 + (tl % 2)
                for k in range(KC):
                    p.op("pe", MM(ps[bank][:, :256], hT[:, k, tl * 128:(tl + 1) * 128], wb[s][:, k, 0:256], k == 0, k == KC - 1),
                         [("wb", s), "hT"], [("ps", bank)])
                for kvh in range(4):
                    for par in range(2):
                        eng = "dve" if par == 0 else "act"
                        fn = CP(Vp[:, kvh * 2 + par, tl, par * 64:(par + 1) * 64], ps[bank][:, kvh * 64:(kvh + 1) * 64]) if par == 0 else \
                            ACT(Vp[:, kvh * 2 + par, tl, par * 64:(par + 1) * 64], ps[bank][:, kvh * 64:(kvh + 1) * 64], AF.Copy)
                        p.op(eng, fn, [("ps", bank)], ["Vp"])
            for b in range(8):
                s = load_w(wblk[3 + b])
                for ii in range(2):
                    i = 2 * b + ii
                    kvh = i // 4
                    for g in range(NG):
                        t0 = HALO + g * 512
                        proj(0, s, ii * 256, 128, t0, 512)
                        p.op("act", ACT(QT[:, g * 512:(g + 1) * 512], ps[0][:], AF.Copy), [("ps", 0)], ["QT"])
                        proj(1, s, ii * 256 + 128, 128, t0, 512)
                        p.op("act", ACT(sgq[:, g * 512:(g + 1) * 512], ps[1][:], AF.Silu), [("ps", 1)], ["sgq"])
                    for par in range(2):
                        p.op("dve", STT(b4[:, par * 256:(par + 1) * 256], rel2[:], float(-slopes[2 * i + par]), mneg2[:], ALU.mult, ALU.add),
                             ["rel2", "mneg2"], ["b4"])
                    if pp == 0:
                        p.op("dve", CP(b4f[:], b4[:]), ["b4"], ["b4f"])
                        for par in range(2):
                            p.op("dve", TS(b4f[:, par * 256:par * 256 + 128], b4f[:, par * 256:par * 256 + 128], hvneg[:, 0:1], ALU.add),
                                 ["b4f", "hvneg"], ["b4f"])
                    for n in range(TP // 128):
                        tl = n + 1
                        sbank = 2 + (n % 2)
                        for par in range(2):
                            for j in range(2):
                                p.op("pe", MM(ps[sbank][:, (par * 2 + j) * 128:(par * 2 + j + 1) * 128],
                                              KTp[:, kvh * 2 + par, (tl - 1 + j) * 128:(tl + j) * 128],
                                              QT[:, n * 128:(n + 1) * 128], True, True),
                                     ["KTp", "QT"], [("ps", sbank)])
                        bsrc = b4f if (pp == 0 and n == 0) else b4
                        bkey = "b4f" if (pp == 0 and n == 0) else "b4"
                        p.op("dve", TT(sb4[:], ps[sbank][:], bsrc[:], ALU.add), [("ps", sbank), bkey], ["sb4"])
                        pi = n % 2
                        p.op("act", ACT(pT[pi][:], sb4[:], AF.Exp), ["sb4"], [("pT", pi)])
                        q4 = n % 4
                        ob, db = 4 + ((n // 4) % 2) * 2, 5 + ((n // 4) % 2) * 2
                        cnt = 0
                        for par in range(2):
                            for j in range(2):
                                p.op("pe", MM(ps[ob][:, q4 * 128:(q4 + 1) * 128], Vp[:, kvh * 2 + par, tl - 1 + j, :],
                                              pT[pi][:, (par * 2 + j) * 128:(par * 2 + j + 1) * 128], cnt == 0, cnt == 3),
                                     ["Vp", ("pT", pi)], [("ps", ob)])
                                cnt += 1
                        cnt = 0
                        for par in range(2):
                            for j in range(2):
                                p.op("pe", MM(ps[db][:, q4 * 128:(q4 + 1) * 128], onesp[par][:],
                                              pT[pi][:, (par * 2 + j) * 128:(par * 2 + j + 1) * 128], cnt == 0, cnt == 3),
                                     [("onesp", par), ("pT", pi)], [("ps", db)])
                                cnt += 1
                        if q4 == 3:
                            g = n // 4
                            p.op("dve", TS(rden[:], ps[db][:], esink[:, i:i + 1], ALU.add), [("ps", db), "esink"], ["rden"])
                            p.op("dve", RECIP(rden[:], rden[:]), ["rden"], ["rden"])
                            p.op("dve", TT(tmp[0][:], ps[ob][:], rden[:], ALU.mult), [("ps", ob), "rden"], [("tmp", 0)])
                            p.op("dve", TT(yg[:, i, g * 512:(g + 1) * 512], tmp[0][:], sgq[:, g * 512:(g + 1) * 512], ALU.mult),
                                 [("tmp", 0), "sgq"], [("yg", i)])

        s = load_w(wblk[nblk - 1])
        for c2 in range(2):
            for g in range(NG):
                t0 = HALO + g * 512
                proj(0 + (g % 2) * 2, s, c2 * 128, 128, t0, 512)
                p.op("act", ACT(qmT[:, c2, g * 512:(g + 1) * 512], ps[0 + (g % 2) * 2][:], AF.Copy), [("ps", 0 + (g % 2) * 2)], ["qmT"])
                proj(1 + (g % 2) * 2, s, 256 + c2 * 128, 128, t0, 512)
                p.op("act", ACT(sgt[:, c2, g * 512:(g + 1) * 512], ps[1 + (g % 2) * 2][:], AF.Silu), [("ps", 1 + (g % 2) * 2)], ["sgt"])
        it = 0
        for c2 in range(2):
            for g in range(NG):
                ob, db = 6, 7
                cnt = 0
                for par in range(2):
                    hm = 2 * c2 + par
                    for j in range(2):
                        sbank = 4 + (it % 2)
                        pi = it % 2
                        it += 1
                        p.op("pe", MM(ps[sbank][:], KmT[:, hm, j * 128:(j + 1) * 128], qmT[:, c2, g * 512:(g + 1) * 512], True, True),
                             ["KmT", "qmT"], [("ps", sbank)])
                        p.op("act", ACT(pT[pi][:], ps[sbank][:], AF.Exp), [("ps", sbank)], [("pT", pi)])
                        p.op("pe", MM(ps[ob][:], Vmp[:, hm, j, :], pT[pi][:], cnt == 0, cnt == 3), ["Vmp", ("pT", pi)], [("ps", ob)])
                        p.op("pe", MM(ps[db][:], onesp[par][:], pT[pi][:], cnt == 0, cnt == 3), [("onesp", par), ("pT", pi)], [("ps", db)])
                        cnt += 1
                p.op("dve", RECIP(tmp[1][:], ps[db][:]), [("ps", db)], [("tmp", 1)])
                p.op("dve", TT(tmp[1][:], ps[ob][:], tmp[1][:], ALU.mult), [("ps", ob), ("tmp", 1)], [("tmp", 1)])
                p.op("dve", TT(yg[:, 16 + c2, g * 512:(g + 1) * 512], tmp[1][:], sgt[:, c2, g * 512:(g + 1) * 512], ALU.mult),
                     [("tmp", 1), "sgt"], [("yg", 16 + c2)])

        ygall = [("yg", c) for c in range(GC)]
        for g in range(NG):
            t0 = HALO + g * 512
            p.dma("sp", xg[:], xT_v[:, :, tb + t0:tb + t0 + 512], [], ["xg"], "xin")
            for m in range(KC):
                bank = m % 4
                for c in range(GC):
                    p.op("pe", MM(ps[bank][:], wo_sb[:, c, m * 128:(m + 1) * 128], yg[:, c, g * 512:(g + 1) * 512], c == 0, c == GC - 1),
                         ["wo_sb"] + ygall, [("ps", bank)])
                p.op("act", ACT(yT[:, m, :], ps[bank][:], AF.Copy), [("ps", bank)], [("yT", m)])
            norm_stats(lambda k: yT[:, k, :], 512, 1.0 / D, 4, [("yT", k) for k in range(KC)])
            for m in range(KC):
                p.op("dve", STT(yT[:, m, :], yT[:, m, :], gpost[:, m:m + 1], rr[:], ALU.mult, ALU.mult),
                     [("yT", m), "gpost", "rr"], [("yT", m)])
                p.op("pool", TT(xg[:, m, :], xg[:, m, :], yT[:, m, :], ALU.add), ["xg", ("yT", m)], ["xg"])
            c0 = pp * TP + g * 512
            p.dma("sp", xoT_v[:, :, c0:c0 + 512], xg[:], ["xg"], ["xo"], "xout")
            if emit_h_next:
                norm_stats(lambda k: xg[:, k, :], 512, 1.0 / D, 5, ["xg"])
                for k in range(KC):
                    p.op("dve", STT(hn[:, k, :], xg[:, k, :], gnext[:, k:k + 1], rr[:], ALU.mult, ALU.mult),
                         ["xg", "gnext", "rr"], ["hn"])
                p.dma("sp", hnT.rearrange("(k p) t -> p k t", p=128)[:, :, c0:c0 + 512], hn[:], ["hn"], ["hno"], "hout")
    fin = ["xo"] + (["hno"] if emit_h_next else [])
    p.op("sp", lambda e: None, fin, [])
    p.emit()
    p.close()
    return nc


def build_diff(S, layer_idx=1):
    NGq = S // 512
    ND = 4 * NGq
    NTL = S // 128
    lam_init = 0.8 - 0.6 * math.exp(-0.3 * layer_idx)
    nc = bass.Bass("TRN2", target_bir_lowering=False)
    dt = nc.dram_tensor
    hT_d = dt("hT", [D, S], BF16, kind="ExternalInput").ap()
    wblk = dt("wblk", [2, 128, KC, 512], F32, kind="ExternalInput").ap()
    tb_d = dt("tbias", [128, 2, ND], F32, kind="ExternalInput").ap()
    qaug_d = dt("qaug", [4, 512], BF16, kind="ExternalInput").ap()
    kaug_d = dt("kaug", [2, 4, S], BF16, kind="ExternalInput").ap()
    mneg_d = dt("mneg", [128, 512], F32, kind="ExternalInput").ap()
    lam_d = dt("lamv", [128, 4, 64], F32, kind="ExternalInput").ap()
    gsub_d = dt("gsub", [128, 1], F32, kind="ExternalInput").ap()
    yo_d = dt("ygT", [256, S], BF16, kind="ExternalOutput").ap()
    hT_v = hT_d.rearrange("(k p) t -> p k t", p=128)

    p = Prog(nc)
    KT = [p.sbuf("KT%d" % c, [68, S], BF16) for c in range(2)]
    V = p.sbuf("V", [128, NTL, 128], BF16)
    hTg = [p.sbuf("hTg%d" % i, [128, KC, 512], BF16) for i in range(2)]
    QTc = [[p.sbuf("QT%d_%d" % (i, c), [68, 512], BF16) for c in range(2)] for i in range(2)]
    wb = [p.sbuf("wb%d" % i, [128, KC, 512], BF16) for i in range(2)]
    PT = [p.sbuf("PT%d" % i, [128, 1024], BF16) for i in range(2)]
    sbm = p.sbuf("sbm", [128, 1024], F32)
    acc = [p.sbuf("acc%d" % c, [128, 512], F32) for c in range(2)]
    sg = p.sbuf("sg", [128, 512], F32)
    rc = [p.sbuf("rc%d" % c, [128, 512], F32) for c in range(2)]
    t0b = p.sbuf("t0b", [128, 512], F32)
    t1b = p.sbuf("t1b", [128, 512], F32)
    ob = p.sbuf("ob", [128, 512], F32)
    sqo = p.sbuf("sqo", [128, 512], BF16)
    rs = p.sbuf("rs", [128, 512], F32)
    yq = [p.sbuf("yq%d" % i, [128, 512], BF16) for i in range(2)]
    tbias = p.sbuf("tbias_s", [128, 2, ND], F32)
    mneg = p.sbuf("mneg_s", [128, 512], F32)
    lamv = p.sbuf("lamv_s", [128, 4, 64], F32)
    lp = p.sbuf("lp", [128, 2, 64], F32)
    ls = p.sbuf("ls", [128, 2], F32)
    nlam = p.sbuf("nlam", [128, 1], F32)
    gsub = p.sbuf("gsub_s", [128, 1], F32)
    ones_b = p.sbuf("ones_b", [128, 128], BF16)
    ones_f = p.sbuf("ones_f", [128, 128], F32)
    psS = [p.psum("psS%d" % i, [128, 1024], F32) for i in range(2)]
    psO = [p.psum("psO%d" % c, [128, 512], F32) for c in range(2)]
    psA = p.psum("psA", [128, 512], F32)
    psB = p.psum("psB", [128, 512], F32)

    p.op("dve", MSET(ones_b[:], 1.0), [], ["ones_b"])
    p.op("dve", MSET(ones_f[:], 1.0), [], ["ones_f"])
    p.dma("sp", tbias[:], tb_d, [], ["tbias"], "c0")
    p.dma("sp", mneg[:], mneg_d, [], ["mneg"], "c1")
    p.dma("sp", lamv[:], lam_d, [], ["lamv"], "c2")
    p.dma("sp", gsub[:], gsub_d, [], ["gsub"], "c3")
    for i in range(2):
        for c in range(2):
            p.dma("sp", QTc[i][c][64:68, :], qaug_d, [], [("QT", i, c)], "c4_%d_%d" % (i, c))
    for j in range(2):
        p.op("dve", TT(lp[:, j, :], lamv[:, 2 * j, :], lamv[:, 2 * j + 1, :], ALU.mult), ["lamv"], ["lp"])
        p.op("dve", lambda e, j=j: e.reduce_sum(ls[:, j:j + 1], lp[:, j, :], axis=mybir.AxisListType.X), ["lp"], ["ls"])
    p.op("act", ACT(ls[:], ls[:], AF.Exp), ["ls"], ["ls"])
    p.op("dve", TT(nlam[:], ls[:, 1:2], ls[:, 0:1], ALU.subtract), ["ls"], ["nlam"])
    p.op("dve", TS(nlam[:], nlam[:], float(-lam_init), ALU.add), ["nlam"], ["nlam"])
    p.op("dve", TS(gsub[:], gsub[:], float(1.0 - lam_init), ALU.mult), ["gsub"], ["gsub"])

    for hh in range(2):
        s = hh
        p.dma("pool", wb[s][:], wblk[hh], [], [("wb", s)], "w%d" % s)
        for c in range(2):
            p.dma("sp", KT[c][64:68, :], kaug_d[hh], [], [("KTaug", c)], "ka%d" % c)
        for g in range(NGq):
            gi = g % 2
            p.dma("sp", hTg[gi][:], hT_v[:, :, g * 512:(g + 1) * 512], [], [("hTg", gi)], "h%d" % gi)
            hk = [("hTg", gi), ("wb", s)]
            for c in range(2):
                for k in range(KC):
                    p.op("pe", MM(psA[:64, :], wb[s][:, k, c * 64:(c + 1) * 64], hTg[gi][:, k, :], k == 0, k == KC - 1), hk, ["psA"])
                p.op("act", ACT(QTc[gi][c][0:64, :], psA[:64, :], AF.Copy, scale=0.125), ["psA"], [("QT", gi, c)])
                for k in range(KC):
                    p.op("pe", MM(psB[:64, :], wb[s][:, k, 128 + c * 64:128 + (c + 1) * 64], hTg[gi][:, k, :], k == 0, k == KC - 1), hk, ["psB"])
                p.op("dve", CP(KT[c][0:64, g * 512:(g + 1) * 512], psB[:64, :]), ["psB"], [("KT", c, g)])
            for t in range(4):
                for k in range(KC):
                    p.op("pe", MM(psA[:, t * 128:(t + 1) * 128], hTg[gi][:, k, t * 128:(t + 1) * 128], wb[s][:, k, 256:384], k == 0, k == KC - 1), hk, ["psA"])
            p.op("dve", CP(V[:, 4 * g:4 * g + 4, :], psA[:].rearrange("p (t e) -> p t e", e=128)), ["psA"], [("V", g)])
            for k in range(KC):
                p.op("pe", MM(psB[:], wb[s][:, k, 384:512], hTg[gi][:, k, :], k == 0, k == KC - 1), hk, ["psB"])
            p.op("act", ACT(sg[:], psB[:], AF.Silu), ["psB"], ["sg"])
            nkb = 4 * g + 4
            for kb in range(nkb):
                j = kb - 4 * g
                q0 = 128 * j if j > 0 else 0
                N = 512 - q0
                si = kb % 2
                delta = 4 * g + 3 - kb
                bias = tbias[:, hh, delta:delta + 1]
                kg = kb // 4
                for c in range(2):
                    p.op("pe", MM(psS[si][:, c * 512 + q0:(c + 1) * 512], KT[c][0:68, kb * 128:(kb + 1) * 128], QTc[gi][c][0:68, q0:512], True, True),
                         [("KT", c, kg), ("KTaug", c), ("QT", gi, c)], [("psS", si, c)])
                if j < 0:
                    p.op("act", ACT(PT[si][:], psS[si][:], AF.Exp, bias=bias), [("psS", si, 0), ("psS", si, 1), "tbias"], [("PT", si, 0), ("PT", si, 1)])
                else:
                    for c in range(2):
                        p.op("dve", TT(sbm[:, c * 512 + q0:(c + 1) * 512], psS[si][:, c * 512 + q0:(c + 1) * 512], mneg[:, 0:N], ALU.add),
                             [("psS", si, c), "mneg"], [("sbm", c)])
                        p.op("act", ACT(PT[si][:, c * 512 + q0:(c + 1) * 512], sbm[:, c * 512 + q0:(c + 1) * 512], AF.Exp, bias=bias),
                             [("sbm", c), "tbias"], [("PT", si, c)])
                for c in range(2):
                    eng = "pool" if c == 0 else "dve"
                    src = PT[si][:, c * 512 + q0:(c + 1) * 512]
                    if kb == 0:
                        p.op(eng, CP(acc[c][:], src), [("PT", si, c)], [("acc", c)])
                    else:
                        p.op(eng, TT(acc[c][:, q0:512], acc[c][:, q0:512], src, ALU.add), [("PT", si, c), ("acc", c)], [("acc", c)])
                    p.op("pe", MM(psO[c][:, q0:512], V[:, kb, :], src, kb == 0, kb == nkb - 1), [("V", kg), ("PT", si, c)], [("psO", c)])
            for c in range(2):
                ps_ = psA if c == 0 else psB
                pk = "psA" if c == 0 else "psB"
                p.op("pe", MM(ps_[:], ones_f[:], acc[c][:], True, True), ["ones_f", ("acc", c)], [pk])
                p.op("dve", RECIP(rc[c][:], ps_[:]), [pk], [("rc", c)])
            p.op("dve", TT(t0b[:], psO[0][:], rc[0][:], ALU.mult), [("psO", 0), ("rc", 0)], ["t0b"])
            p.op("dve", TT(t1b[:], psO[1][:], rc[1][:], ALU.mult), [("psO", 1), ("rc", 1)], ["t1b"])
            p.op("dve", STT(ob[:], t1b[:], nlam[:, 0:1], t0b[:], ALU.mult, ALU.add), ["t0b", "t1b", "nlam"], ["ob"])
            p.op("pool", TT(sqo[:], ob[:], ob[:], ALU.mult), ["ob"], ["sqo"])
            p.op("pe", MM(psA[:], ones_b[:], sqo[:], True, True), ["ones_b", "sqo"], ["psA"])
            p.op("act", ACT(rs[:], psA[:], AF.Ln, bias=EPS, scale=1.0 / 128), ["psA"], ["rs"])
            p.op("act", ACT(rs[:], rs[:], AF.Exp, scale=-0.5), ["rs"], ["rs"])
            p.op("dve", STT(ob[:], ob[:], gsub[:, 0:1], rs[:], ALU.mult, ALU.mult), ["ob", "gsub", "rs"], ["ob"])
            yi = g % 2
            p.op("dve", TT(yq[yi][:], ob[:], sg[:], ALU.mult), ["ob", "sg"], [("yq", yi)])
            p.dma("sp", yo_d[hh * 128:(hh + 1) * 128, g * 512:(g + 1) * 512], yq[yi][:], [("yq", yi)], ["yo"], "yo%d" % yi)
    p.op("sp", lambda e: None, ["yo"], [])
    p.emit()
    p.close()
    return nc


_PROGS = {}
_DBG = {}


def _prog(key, fn):
    if key not in _PROGS:
        _PROGS[key] = fn()
    return _PROGS[key]


def _launch(nc, in_maps):
    res = run_bass_kernel_spmd(nc, in_maps, core_ids=list(range(NCORES)))
    return res.results


def _tok_common(inp, i, memT):
    f = lambda a: np.asarray(a, np.float32)
    return {
        "memT": memT,
        "wkv": arrange_blocks(f(inp["w_mem_kv_%d" % i]), memkv_colsets()),
        "wo": arrange_wout(f(inp["w_out_%d" % i])),
        "gpre": arrange_vec(inp["norm_pre_%d" % i], KC),
        "gpost": arrange_vec(inp["norm_post_%d" % i], KC),
        "gmem": arrange_vec(inp["norm_mem_%d" % i], KC),
    }


def _halo_split(xT_full, TC, HALO):
    S = xT_full.shape[1]
    outs = []
    for c in range(NCORES):
        own = xT_full[:, c * TC:(c + 1) * TC]
        if HALO == 0:
            outs.append(np.ascontiguousarray(own))
            continue
        if c == 0:
            h = np.zeros((D, HALO), np.float32)
        else:
            h = xT_full[:, c * TC - HALO:c * TC]
        outs.append(np.ascontiguousarray(np.concatenate([h, own], axis=1)))
    return outs


def run_model(inp, S, stop_after=None):
    f = lambda a: np.asarray(a, np.float32)
    TC = S // NCORES
    TPc = min(1024, TC)
    TPs = min(512, TC)
    x = f(inp["x"])[0]
    xT_full = np.ascontiguousarray(x.T)
    memT = np.ascontiguousarray(f(inp["mem"])[0].T)

    def conv_layer(i, xT_full, emit_next):
        nc = _prog(("conv", TC, TPc, emit_next), lambda: build_tok("conv", TC, TPc, emit_h_next=emit_next))
        com = _tok_common(inp, i, memT)
        com["wblk"] = arrange_blocks(f(inp["w_in_%d" % i]), conv_colsets())
        cwv = f(inp["conv_w_%d" % i])
        com["cw"] = np.ascontiguousarray(cwv.reshape(3, 16, 128).transpose(2, 1, 0))
        if emit_next:
            com["gnext"] = arrange_vec(inp["norm_pre_%d" % (i + 1)], KC)
        xs = _halo_split(xT_full, TC, 16)
        res = _launch(nc, [dict(com, xT=xs[c]) for c in range(NCORES)])
        xo = np.concatenate([r["xoT"] for r in res], axis=1)
        hn = np.concatenate([np.asarray(r["hnT"]) for r in res], axis=1) if emit_next else None
        return xo, hn

    x1T, h1T = conv_layer(0, xT_full, True)
    if stop_after == 0:
        return x1T, h1T

    i = 1
    nc = _prog(("diff", S), lambda: build_diff(S, 1))
    slopes = alibi_slopes(16).astype(np.float64)
    NGq = S // 512
    ND = 4 * NGq
    w1 = f(inp["w_in_1"])
    ql = np.arange(512)
    qaug = bf(np.stack([ql // 16, ql % 16, ql // 16, ql % 16]).astype(np.float32))
    mneg = np.zeros((128, 512), np.float32)
    mneg[:, :128] = np.where(np.arange(128)[None, :] >= np.arange(128)[:, None], 0.0, NEG)
    lamv = np.stack([f(inp["lambda_q1_1"]), f(inp["lambda_k1_1"]), f(inp["lambda_q2_1"]), f(inp["lambda_k2_1"])])
    lamv = np.ascontiguousarray(np.broadcast_to(lamv[None], (128, 4, 64)))
    gsub = np.ascontiguousarray(f(inp["subln_1"]).reshape(128, 1))
    h1T = np.ascontiguousarray(h1T)
    maps = []
    for c in range(NCORES):
        colsets = []
        tb = np.zeros((128, 2, ND), np.float32)
        kaug = np.zeros((2, 4, S), np.float32)
        for hh in range(2):
            h = 2 * c + hh
            cs = []
            for base in (0, 2048):
                for cm in range(2):
                    cs += list(range(base + h * 128 + cm * 64, base + h * 128 + cm * 64 + 64))
            cs += list(range(4096 + h * 128, 4096 + h * 128 + 128))
            cs += list(range(6400 + h * 128, 6400 + h * 128 + 128))
            colsets.append(cs)
            sl = float(slopes[h])
            dl = np.arange(ND)[None, :]
            tb[:, hh, :] = (sl * (128.0 * (3 - dl) + np.arange(128)[:, None])).astype(np.float32)
            s_hi = float(np.float32(sl).astype(ml_dtypes.bfloat16))
            s_lo = float(np.float32(sl - s_hi).astype(ml_dtypes.bfloat16))
            kaug[hh] = np.array([-16 * s_hi, -s_hi, -16 * s_lo, -s_lo], np.float32)[:, None]
        maps.append({"hT": h1T, "wblk": arrange_blocks(w1, colsets), "tbias": tb, "qaug": qaug,
                     "kaug": bf(kaug), "mneg": mneg, "lamv": lamv, "gsub": gsub})
    res = _launch(nc, maps)
    mixT_full = np.concatenate([np.asarray(r["ygT"]) for r in res], axis=0)
    if stop_after == "1a":
        return mixT_full

    nc = _prog(("diffpost", TC, TPc), lambda: build_tok("diffpost", TC, TPc))
    com = _tok_common(inp, 1, memT)
    com["wblk"] = arrange_blocks(w1, diffpost_colsets())
    res = _launch(nc, [dict(com, xT=np.ascontiguousarray(x1T[:, c * TC:(c + 1) * TC]),
                            mixT=np.ascontiguousarray(mixT_full[:, c * TC:(c + 1) * TC])) for c in range(NCORES)])
    x2T = np.concatenate([r["xoT"] for r in res], axis=1)
    _DBG["x2T"] = x2T
    if stop_after == 1:
        return x2T

    nc = _prog(("swa", TC, TPs), lambda: build_tok("swa", TC, TPs))
    com = _tok_common(inp, 2, memT)
    com["wblk"] = arrange_blocks(f(inp["w_in_2"]), swa_colsets())
    sidx = np.arange(128)[:, None]
    qidx = np.arange(128)[None, :]
    rel = np.concatenate([qidx - sidx + 128, qidx - sidx], axis=1).astype(np.float32)
    com["rel2"] = rel
    com["mneg2"] = np.where((rel >= 0) & (rel < 128), 0.0, NEG).astype(np.float32)
    sk = f(inp["sinks_2"])
    com["sinks"] = np.ascontiguousarray(sk.reshape(16, 2)[:, (np.arange(128) // 64)].T)
    xs = _halo_split(x2T, TC, 128)
    maps = []
    for c in range(NCORES):
        hv = np.full((128, 1), NEG if c == 0 else 0.0, np.float32)
        maps.append(dict(com, xT=xs[c], hvneg=hv))
    res = _launch(nc, maps)
    x3T = np.concatenate([r["xoT"] for r in res], axis=1)
    _DBG["x3T"] = x3T
    if stop_after == 2:
        return x3T

    x4T, _ = conv_layer(3, x3T, False)
    return x4T


def kernel(**inputs):
    S = inputs["x"].shape[1]
    outT = run_model(inputs, S)
    return np.ascontiguousarray(outT.T)[None].astype(np.float32)
```

```python
import contextlib
import math
import numpy as np
import ml_dtypes
import concourse.bass as bass
import concourse.mybir as mybir
from concourse.bass_utils import run_bass_kernel_spmd

F32 = mybir.dt.float32
BF16 = mybir.dt.bfloat16
AF = mybir.ActivationFunctionType
ALU = mybir.AluOpType

NCORES = 8
D = 1024
KC = 8
BR = 2048
GW = 2304
GC = 18
EPS = 1e-6
NEG = -1e30
SAME_ENGINE_SYNC = True


class _Op:
    __slots__ = ("eng", "fn", "reads", "writes", "stream", "deps", "signal", "sig_sem", "sig_val")

    def __init__(self, eng, fn, reads, writes, stream):
        self.eng = eng
        self.fn = fn
        self.reads = tuple(reads)
        self.writes = tuple(writes)
        self.stream = stream
        self.deps = ()
        self.signal = False
        self.sig_sem = None
        self.sig_val = 0


class Prog:
    ENGS = ("pe", "act", "dve", "pool", "sp")

    def __init__(self, nc):
        self.nc = nc
        self.ops = []
        self.stack = contextlib.ExitStack()

    def sbuf(self, name, shape, dtype):
        return self.stack.enter_context(self.nc.sbuf_tensor(name, list(shape), dtype))

    def psum(self, name, shape, dtype):
        return self.stack.enter_context(self.nc.psum_tensor(name, list(shape), dtype))

    def op(self, eng, fn, reads=(), writes=(), stream=None):
        o = _Op(eng, fn, reads, writes, stream)
        self.ops.append(o)
        return o

    def dma(self, q, out, in_, reads, writes, stream):
        return self.op(q, lambda e: e.dma_start(out=out, in_=in_), reads, writes, stream)

    def emit(self):
        nc = self.nc
        ops = self.ops
        last_writer = {}
        readers = {}
        for i, o in enumerate(ops):
            deps = set()
            for b in o.reads:
                w = last_writer.get(b)
                if w is not None:
                    deps.add(w)
            for b in o.writes:
                w = last_writer.get(b)
                if w is not None:
                    deps.add(w)
                r = readers.get(b)
                if r:
                    deps.update(r)
            deps.discard(i)
            o.deps = deps
            for b in o.reads:
                readers.setdefault(b, []).append(i)
            for b in o.writes:
                last_writer[b] = i
                readers[b] = []
        for o in ops:
            for d in o.deps:
                od = ops[d]
                if od.stream is not None or od.eng != o.eng or (SAME_ENGINE_SYNC and od.eng != "pe"):
                    od.signal = True
        for o in ops:
            if o.stream is not None:
                o.signal = True
        sems = {}
        counts = {}
        for o in ops:
            if not o.signal:
                continue
            key = ("d", o.stream) if o.stream is not None else ("e", o.eng)
            if key not in sems:
                sems[key] = self.stack.enter_context(nc.semaphore("s_%s_%s" % key))
                counts[key] = 0
            counts[key] += 16 if o.stream is not None else 1
            o.sig_sem = key
            o.sig_val = counts[key]
        self.sem_counts = dict(counts)
        per_eng = {e: [o for o in ops if o.eng == e] for e in self.ENGS}
        waited = {}

        def run(engname, e):
            for o in per_eng[engname]:
                need = {}
                for d in o.deps:
                    od = ops[d]
                    if not od.signal:
                        continue
                    if od.stream is None and od.eng == engname and not (SAME_ENGINE_SYNC and engname != "pe"):
                        continue
                    k = od.sig_sem
                    if od.sig_val > need.get(k, 0):
                        need[k] = od.sig_val
                for k, v in need.items():
                    if waited.get((engname, k), 0) >= v:
                        continue
                    waited[(engname, k)] = v
                    e.wait_ge(sems[k], v)
                ins = o.fn(e)
                if o.signal:
                    ins.then_inc(sems[o.sig_sem], 16 if o.stream is not None else 1)

        with nc.Block() as block:
            @block.tensor
            def _(e):
                run("pe", e)

            @block.scalar
            def _(e):
                run("act", e)

            @block.vector
            def _(e):
                run("dve", e)

            @block.gpsimd
            def _(e):
                run("pool", e)

            @block.sync
            def _(e):
                run("sp", e)

    def close(self):
        self.stack.close()


def MM(out, lhsT, rhs, start, stop):
    return lambda e: e.matmul(out, lhsT, rhs, start=start, stop=stop)


def ACT(out, in_, func, bias=None, scale=None):
    kw = {}
    if bias is not None:
        kw["bias"] = bias
    if scale is not None:
        kw["scale"] = scale
    return lambda e: e.activation(out, in_, func, **kw)


def TT(out, a, b, op):
    return lambda e: e.tensor_tensor(out, a, b, op)


def TS(out, a, s1, op0, s2=None, op1=None):
    if op1 is None:
        return lambda e: e.tensor_scalar(out, a, s1, None, op0)
    return lambda e: e.tensor_scalar(out, a, s1, s2, op0, op1)


def STT(out, in0, scalar, in1, op0, op1):
    return lambda e: e.scalar_tensor_tensor(out, in0, scalar, in1, op0, op1)


def CP(out, in_):
    return lambda e: e.tensor_copy(out, in_)


def MSET(out, v):
    return lambda e: e.memset(out, v)


def RECIP(out, in_):
    return lambda e: e.reciprocal(out, in_)


def alibi_slopes(n):
    return (2.0 ** (-8.0 * np.arange(1, n + 1, dtype=np.float64) / n)).astype(np.float32)


def bf(x):
    return np.ascontiguousarray(np.asarray(x).astype(ml_dtypes.bfloat16))


def arrange_blocks(W, colsets):
    nb = len(colsets)
    out = np.zeros((nb, 128, KC, 512), np.float32)
    Wr = W.reshape(KC, 128, -1)
    for b, cs in enumerate(colsets):
        cs = np.asarray(cs)
        ok = cs >= 0
        blk = np.zeros((KC, 128, 512), np.float32)
        blk[:, :, ok] = Wr[:, :, cs[ok]]
        out[b] = blk.transpose(1, 0, 2)
    return out


def arrange_wout(Wo):
    return np.ascontiguousarray(Wo.reshape(GC, 128, D).transpose(1, 0, 2))


def arrange_vec(v, n):
    return np.ascontiguousarray(np.asarray(v, np.float32).reshape(n, 128).T)


def tail_cols(q0, g0):
    return list(range(q0, q0 + 256)) + list(range(g0 + 2048, g0 + 2304))


def conv_colsets():
    cs = []
    for fc in range(16):
        c = []
        for base in (0, 2048, 4096, 6400):
            c += list(range(base + fc * 128, base + fc * 128 + 128))
        cs.append(c)
    cs.append(tail_cols(6144, 6400))
    return cs


def swa_colsets():
    cs = []
    for kb in range(2):
        c = []
        for kvh in (2 * kb, 2 * kb + 1):
            for par in range(2):
                blk = [-1] * 128
                for d in range(64):
                    blk[par * 64 + d] = 2048 + kvh * 64 + d
                c += blk
        cs.append(c)
    cs.append(list(range(2304, 2560)) + [-1] * 256)
    for b in range(8):
        c = []
        for i in (2 * b, 2 * b + 1):
            c += list(range(i * 128, i * 128 + 128))
            c += list(range(2816 + i * 128, 2816 + i * 128 + 128))
        cs.append(c)
    cs.append(tail_cols(2560, 2816))
    return cs


def diffpost_colsets():
    return [tail_cols(6144, 6400)]


def memkv_colsets():
    cs = []
    c = []
    for hm in range(4):
        blk = [-1] * 128
        for d in range(64):
            blk[(hm % 2) * 64 + d] = hm * 64 + d
        c += blk
    cs.append(c)
    cs.append(list(range(256, 512)) + [-1] * 256)
    return cs


def build_tok(kind, TC, TP, emit_h_next=False):
    HALO = {"conv": 16, "swa": 128, "diffpost": 0}[kind]
    NP = TC // TP
    NG = TP // 512
    nblk = {"conv": 17, "swa": 12, "diffpost": 1}[kind]
    nc = bass.Bass("TRN2", target_bir_lowering=False)
    dt = nc.dram_tensor
    xT = dt("xT", [D, HALO + TC], F32, kind="ExternalInput").ap()
    memT = dt("memT", [D, 256], F32, kind="ExternalInput").ap()
    wblk = dt("wblk", [nblk, 128, KC, 512], F32, kind="ExternalInput").ap()
    wkv = dt("wkv", [2, 128, KC, 512], F32, kind="ExternalInput").ap()
    wo = dt("wo", [128, GC, D], F32, kind="ExternalInput").ap()
    gpre_d = dt("gpre", [128, KC], F32, kind="ExternalInput").ap()
    gpost_d = dt("gpost", [128, KC], F32, kind="ExternalInput").ap()
    gmem_d = dt("gmem", [128, KC], F32, kind="ExternalInput").ap()
    if emit_h_next:
        gnext_d = dt("gnext", [128, KC], F32, kind="ExternalInput").ap()
        hnT = dt("hnT", [D, TC], BF16, kind="ExternalOutput").ap()
    if kind == "conv":
        cw_d = dt("cw", [128, 16, 3], F32, kind="ExternalInput").ap()
    if kind == "swa":
        rel_d = dt("rel2", [128, 256], F32, kind="ExternalInput").ap()
        mneg_d = dt("mneg2", [128, 256], F32, kind="ExternalInput").ap()
        hv_d = dt("hvneg", [128, 1], F32, kind="ExternalInput").ap()
        sk_d = dt("sinks", [128, 16], F32, kind="ExternalInput").ap()
    if kind == "diffpost":
        mixT = dt("mixT", [BR, TC], BF16, kind="ExternalInput").ap()
    xoT = dt("xoT", [D, TC], F32, kind="ExternalOutput").ap()

    xT_v = xT.rearrange("(k p) t -> p k t", p=128)
    xoT_v = xoT.rearrange("(k p) t -> p k t", p=128)
    memT_v = memT.rearrange("(k p) t -> p k t", p=128)

    p = Prog(nc)
    NT = HALO + TP
    hT = p.sbuf("hT", [128, KC, NT], BF16)
    yg = p.sbuf("yg", [128, GC, TP], BF16)
    wo_sb = p.sbuf("wo_sb", [128, GC, D], BF16)
    xg = p.sbuf("xg", [128, KC, 512], F32)
    yT = p.sbuf("yT", [128, KC, 512], F32)
    NW = 3
    wb = [p.sbuf("wb%d" % i, [128, KC, 512], BF16) for i in range(NW)]
    sq = [p.sbuf("sq%d" % i, [128, 512], BF16) for i in range(2)]
    rr = p.sbuf("rr", [128, 512], F32)
    tmp = [p.sbuf("tmp%d" % i, [128, 512], F32) for i in range(2)]
    ones_b = p.sbuf("ones_b", [128, 128], BF16)
    onesp = [p.sbuf("onesp%d" % i, [128, 128], BF16) for i in range(2)]
    gpre = p.sbuf("gpre_s", [128, KC], F32)
    gpost = p.sbuf("gpost_s", [128, KC], F32)
    gmem = p.sbuf("gmem_s", [128, KC], F32)
    mnT = p.sbuf("mnT", [128, KC, 256], BF16)
    KmT = p.sbuf("KmT", [128, 4, 256], BF16)
    Vmp = p.sbuf("Vmp", [128, 4, 2, 128], BF16)
    qmT = p.sbuf("qmT", [128, 2, TP], BF16)
    sgt = p.sbuf("sgt", [128, 2, TP], BF16)
    pT = [p.sbuf("pT%d" % i, [128, 512], BF16) for i in range(2)]
    if emit_h_next:
        gnext = p.sbuf("gnext_s", [128, KC], F32)
        hn = p.sbuf("hn", [128, KC, 512], BF16)
    if kind == "conv":
        cw = p.sbuf("cw_s", [128, 16, 3], F32)
        zb = p.sbuf("zb", [128, HALO + TP], F32)
        u_sb = p.sbuf("u_sb", [128, 512], F32)
        sg = p.sbuf("sg", [128, 512], F32)
        cb = p.sbuf("cb", [128, 512], F32)
    if kind == "swa":
        rel2 = p.sbuf("rel2_s", [128, 256], F32)
        mneg2 = p.sbuf("mneg2_s", [128, 256], F32)
        hvneg = p.sbuf("hv_s", [128, 1], F32)
        sinks = p.sbuf("sinks_s", [128, 16], F32)
        esink = p.sbuf("esink", [128, 16], F32)
        NTL = NT // 128
        KTp = p.sbuf("KTp", [128, 8, NT], BF16)
        Vp = p.sbuf("Vp", [128, 8, NTL, 128], BF16)
        QT = p.sbuf("QT", [128, TP], BF16)
        sgq = p.sbuf("sgq", [128, TP], F32)
        b4 = p.sbuf("b4", [128, 512], F32)
        b4f = p.sbuf("b4f", [128, 512], F32)
        sb4 = p.sbuf("sb4", [128, 512], F32)
        rden = p.sbuf("rden", [128, 512], F32)
    ps = [p.psum("ps%d" % i, [128, 512], F32) for i in range(8)]

    p.op("dve", MSET(ones_b[:], 1.0), [], ["ones_b"])
    for par in range(2):
        p.op("dve", MSET(onesp[par][:], 0.0), [], [("onesp", par)])
        p.op("dve", MSET(onesp[par][:, par * 64:(par + 1) * 64], 1.0), [], [("onesp", par)])
    p.dma("sp", gpre[:], gpre_d, [], ["gpre"], "c0")
    p.dma("sp", gpost[:], gpost_d, [], ["gpost"], "c1")
    p.dma("sp", gmem[:], gmem_d, [], ["gmem"], "c2")
    if emit_h_next:
        p.dma("sp", gnext[:], gnext_d, [], ["gnext"], "c3")
    if kind == "conv":
        p.dma("sp", cw[:], cw_d, [], ["cw"], "c4")
    if kind == "swa":
        p.dma("sp", rel2[:], rel_d, [], ["rel2"], "c4")
        p.dma("sp", mneg2[:], mneg_d, [], ["mneg2"], "c5")
        p.dma("sp", hvneg[:], hv_d, [], ["hvneg"], "c6")
        p.dma("sp", sinks[:], sk_d, [], ["sinks"], "c7")
        p.op("act", ACT(esink[:], sinks[:], AF.Exp), ["sinks"], ["esink"])
        p.op("pool", MSET(Vp[:], 0.0), [], ["Vp"])
    p.op("pool", MSET(Vmp[:], 0.0), [], ["Vmp"])

    wstate = {"n": 0}

    def load_w(src_ap):
        s = wstate["n"] % NW
        wstate["n"] += 1
        p.dma("pool", wb[s][:], src_ap, [], [("wb", s)], "w%d" % s)
        return s

    sqi = {"n": 0}

    def norm_stats(src_of_k, N, nparts_scale, bank, tagreads):
        for k in range(KC):
            s = sqi["n"] % 2
            sqi["n"] += 1
            src = src_of_k(k)
            p.op("pool", TT(sq[s][:, :N], src, src, ALU.mult), tagreads, [("sq", s)])
            p.op("pe", MM(ps[bank][:, :N], ones_b[:], sq[s][:, :N], k == 0, k == KC - 1),
                 ["ones_b", ("sq", s)], [("ps", bank)])
        p.op("act", ACT(rr[:, :N], ps[bank][:, :N], AF.Ln, bias=EPS, scale=nparts_scale), [("ps", bank)], ["rr"])
        p.op("act", ACT(rr[:, :N], rr[:, :N], AF.Exp, scale=-0.5), ["rr"], ["rr"])

    def proj(bank, slot, c0, M, t0, N, pkey=None):
        for k in range(KC):
            p.op("pe", MM(ps[bank][:M, :N], wb[slot][:, k, c0:c0 + M], hT[:, k, t0:t0 + N], k == 0, k == KC - 1),
                 [("wb", slot), "hT"], [("ps", bank)])

    p.dma("sp", xg[:, :, 0:256], memT_v, [], ["xg"], "xin")
    norm_stats(lambda k: xg[:, k, 0:256], 256, 1.0 / D, 0, ["xg"])
    for k in range(KC):
        p.op("dve", STT(mnT[:, k, :], xg[:, k, 0:256], gmem[:, k:k + 1], rr[:, 0:256], ALU.mult, ALU.mult),
             ["xg", "gmem", "rr"], ["mnT"])
    sk = load_w(wkv[0])
    sv = load_w(wkv[1])
    for hm in range(4):
        bank = 1 + (hm % 2)
        for k in range(KC):
            p.op("pe", MM(ps[bank][:, :256], wb[sk][:, k, hm * 128:(hm + 1) * 128], mnT[:, k, :], k == 0, k == KC - 1),
                 [("wb", sk), "mnT"], [("ps", bank)])
        p.op("act", ACT(KmT[:, hm, :], ps[bank][:, :256], AF.Copy, scale=0.125), [("ps", bank)], ["KmT"])
    for t in range(2):
        bank = 3 + t
        for k in range(KC):
            p.op("pe", MM(ps[bank][:, :256], mnT[:, k, t * 128:(t + 1) * 128], wb[sv][:, k, 0:256], k == 0, k == KC - 1),
                 [("wb", sv), "mnT"], [("ps", bank)])
        for hm in range(4):
            par = hm % 2
            p.op("dve", CP(Vmp[:, hm, t, par * 64:(par + 1) * 64], ps[bank][:, hm * 64:(hm + 1) * 64]),
                 [("ps", bank)], ["Vmp"])

    for pp in range(NP):
        tb = pp * TP
        if pp == 0:
            for c3 in range(3):
                p.dma("pool", wo_sb[:, c3 * 6:(c3 + 1) * 6, :], wo[:, c3 * 6:(c3 + 1) * 6, :], [], ["wo_sb"], "wo")
        groups = []
        if HALO:
            groups.append((0, HALO))
        for g in range(NG):
            groups.append((HALO + g * 512, 512))
        for (t0, N) in groups:
            p.dma("sp", xg[:, :, 0:N], xT_v[:, :, tb + t0:tb + t0 + N], [], ["xg"], "xin")
            norm_stats(lambda k, N=N: xg[:, k, 0:N], N, 1.0 / D, 0, ["xg"])
            for k in range(KC):
                p.op("dve", STT(hT[:, k, t0:t0 + N], xg[:, k, 0:N], gpre[:, k:k + 1], rr[:, 0:N], ALU.mult, ALU.mult),
                     ["xg", "gpre", "rr"], ["hT"])

        if kind == "conv":
            for fc in range(16):
                s = load_w(wblk[fc])
                for (sec, bank) in ((1, 0), (2, 1)):
                    proj(bank, s, sec * 128, 128, 0, HALO)
                p.op("act", ACT(u_sb[:, :HALO], ps[1][:, :HALO], AF.Copy), [("ps", 1)], ["u_sb"])
                p.op("dve", TT(zb[:, 0:HALO], ps[0][:, :HALO], u_sb[:, :HALO], ALU.mult), [("ps", 0), "u_sb"], [("zb", -1)])
                for g in range(NG):
                    t0 = HALO + g * 512
                    bb = 4 * (g % 2)
                    for sec in range(4):
                        proj(bb + sec, s, sec * 128, 128, t0, 512)
                    p.op("act", ACT(u_sb[:], ps[bb + 2][:], AF.Copy), [("ps", bb + 2)], ["u_sb"])
                    p.op("act", ACT(sg[:], ps[bb + 3][:], AF.Silu), [("ps", bb + 3)], ["sg"])
                    p.op("dve", TT(zb[:, t0:t0 + 512], ps[bb + 1][:], u_sb[:], ALU.mult), [("ps", bb + 1), "u_sb"], [("zb", g)])
                    zr = [("zb", g - 1), ("zb", g)]
                    p.op("pool", TS(cb[:], zb[:, t0 - 2:t0 + 510], cw[:, fc, 0:1], ALU.mult), zr + ["cw"], ["cb"])
                    p.op("dve", STT(cb[:], zb[:, t0 - 1:t0 + 511], cw[:, fc, 1:2], cb[:], ALU.mult, ALU.add), zr + ["cw", "cb"], ["cb"])
                    p.op("dve", STT(cb[:], zb[:, t0:t0 + 512], cw[:, fc, 2:3], cb[:], ALU.mult, ALU.add), zr + ["cw", "cb"], ["cb"])
                    p.op("dve", TT(tmp[0][:], ps[bb][:], cb[:], ALU.mult), [("ps", bb), "cb"], [("tmp", 0)])
                    p.op("dve", TT(yg[:, fc, g * 512:(g + 1) * 512], tmp[0][:], sg[:], ALU.mult), [("tmp", 0), "sg"], [("yg", fc)])
        elif kind == "diffpost":
            mix_v = mixT.rearrange("(c p) t -> p c t", p=128)
            for c4 in range(4):
                p.dma("sp", yg[:, c4 * 4:(c4 + 1) * 4, :], mix_v[:, c4 * 4:(c4 + 1) * 4, tb:tb + TP], [],
                      [("yg", c4 * 4 + i) for i in range(4)], "mixin%d" % c4)
        elif kind == "swa":
            slopes = alibi_slopes(32)
            for kb in range(2):
                s = load_w(wblk[kb])
                for q in range(4):
                    idx = kb * 4 + q
                    for (t0, N) in groups:
                        bank = q % 4
                        proj(bank, s, q * 128, 128, t0, N)
                        p.op("act", ACT(KTp[:, idx, t0:t0 + N], ps[bank][:, :N], AF.Copy, scale=0.125), [("ps", bank)], ["KTp"])
            s = load_w(wblk[2])
            for tl in range(NTL):
                bank = 4 + (tl % 2)
                for k in range(KC):
                    p.op("pe", MM(ps[bank][:, :256], hT[:, k, tl * 128:(tl + 1) * 128], wb[s][:, k, 0:256], k == 0, k == KC - 1),
                         [("wb", s), "hT"], [("ps", bank)])
                for kvh in range(4):
                    for par in range(2):
                        eng = "dve" if par == 0 else "act"
                        fn = CP(Vp[:, kvh * 2 + par, tl, par * 64:(par + 1) * 64], ps[bank][:, kvh * 64:(kvh + 1) * 64]) if par == 0 else \
                            ACT(Vp[:, kvh * 2 + par, tl, par * 64:(par + 1) * 64], ps[bank][:, kvh * 64:(kvh + 1) * 64], AF.Copy)
                        p.op(eng, fn, [("ps", bank)], ["Vp"])
            for b in range(8):
                s = load_w(wblk[3 + b])
                for ii in range(2):
                    i = 2 * b + ii
                    kvh = i // 4
                    for g in range(NG):
                        t0 = HALO + g * 512
                        proj(0, s, ii * 256, 128, t0, 512)
                        p.op("act", ACT(QT[:, g * 512:(g + 1) * 512], ps[0][:], AF.Copy), [("ps", 0)], ["QT"])
                        proj(1, s, ii * 256 + 128, 128, t0, 512)
                        p.op("act", ACT(sgq[:, g * 512:(g + 1) * 512], ps[1][:], AF.Silu), [("ps", 1)], ["sgq"])
                    for par in range(2):
                        p.op("dve", STT(b4[:, par * 256:(par + 1) * 256], rel2[:], float(-slopes[2 * i + par]), mneg2[:], ALU.mult, ALU.add),
                             ["rel2", "mneg2"], ["b4"])
                    if pp == 0:
                        p.op("dve", CP(b4f[:], b4[:]), ["b4"], ["b4f"])
                        for par in range(2):
                            p.op("dve", TS(b4f[:, par * 256:par * 256 + 128], b4f[:, par * 256:par * 256 + 128], hvneg[:, 0:1], ALU.add),
                                 ["b4f", "hvneg"], ["b4f"])
                    for n in range(TP // 128):
                        tl = n + 1
                        sbank = 2 + (n % 2)
                        for par in range(2):
                            for j in range(2):
                                p.op("pe", MM(ps[sbank][:, (par * 2 + j) * 128:(par * 2 + j + 1) * 128],
                                              KTp[:, kvh * 2 + par, (tl - 1 + j) * 128:(tl + j) * 128],
                                              QT[:, n * 128:(n + 1) * 128], True, True),
                                     ["KTp", "QT"], [("ps", sbank)])
                        bsrc = b4f if (pp == 0 and n == 0) else b4
                        bkey = "b4f" if (pp == 0 and n == 0) else "b4"
                        p.op("dve", TT(sb4[:], ps[sbank][:], bsrc[:], ALU.add), [("ps", sbank), bkey], ["sb4"])
                        pi = n % 2
                        p.op("act", ACT(pT[pi][:], sb4[:], AF.Exp), ["sb4"], [("pT", pi)])
                        q4 = n % 4
                        ob, db = 4 + ((n // 4) % 2) * 2, 5 + ((n // 4) % 2) * 2
                        cnt = 0
                        for par in range(2):
                            for j in range(2):
                                p.op("pe", MM(ps[ob][:, q4 * 128:(q4 + 1) * 128], Vp[:, kvh * 2 + par, tl - 1 + j, :],
                                              pT[pi][:, (par * 2 + j) * 128:(par * 2 + j + 1) * 128], cnt == 0, cnt == 3),
                                     ["Vp", ("pT", pi)], [("ps", ob)])
                                cnt += 1
                        cnt = 0
                        for par in range(2):
                            for j in range(2):
                                p.op("pe", MM(ps[db][:, q4 * 128:(q4 + 1) * 128], onesp[par][:],
                                              pT[pi][:, (par * 2 + j) * 128:(par * 2 + j + 1) * 128], cnt == 0, cnt == 3),
                                     [("onesp", par), ("pT", pi)], [("ps", db)])
                                cnt += 1
                        if q4 == 3:
                            g = n // 4
                            p.op("dve", TS(rden[:], ps[db][:], esink[:, i:i + 1], ALU.add), [("ps", db), "esink"], ["rden"])
                            p.op("dve", RECIP(rden[:], rden[:]), ["rden"], ["rden"])
                            p.op("dve", TT(tmp[0][:], ps[ob][:], rden[:], ALU.mult), [("ps", ob), "rden"], [("tmp", 0)])
                            p.op("dve", TT(yg[:, i, g * 512:(g + 1) * 512], tmp[0][:], sgq[:, g * 512:(g + 1) * 512], ALU.mult),
                                 [("tmp", 0), "sgq"], [("yg", i)])

        s = load_w(wblk[nblk - 1])
        for c2 in range(2):
            for g in range(NG):
                t0 = HALO + g * 512
                proj(0 + (g % 2) * 2, s, c2 * 128, 128, t0, 512)
                p.op("act", ACT(qmT[:, c2, g * 512:(g + 1) * 512], ps[0 + (g % 2) * 2][:], AF.Copy), [("ps", 0 + (g % 2) * 2)], ["qmT"])
                proj(1 + (g % 2) * 2, s, 256 + c2 * 128, 128, t0, 512)
                p.op("act", ACT(sgt[:, c2, g * 512:(g + 1) * 512], ps[1 + (g % 2) * 2][:], AF.Silu), [("ps", 1 + (g % 2) * 2)], ["sgt"])
        it = 0
        for c2 in range(2):
            for g in range(NG):
                ob, db = 6, 7
                cnt = 0
                for par in range(2):
                    hm = 2 * c2 + par
                    for j in range(2):
                        sbank = 4 + (it % 2)
                        pi = it % 2
                        it += 1
                        p.op("pe", MM(ps[sbank][:], KmT[:, hm, j * 128:(j + 1) * 128], qmT[:, c2, g * 512:(g + 1) * 512], True, True),
                             ["KmT", "qmT"], [("ps", sbank)])
                        p.op("act", ACT(pT[pi][:], ps[sbank][:], AF.Exp), [("ps", sbank)], [("pT", pi)])
                        p.op("pe", MM(ps[ob][:], Vmp[:, hm, j, :], pT[pi][:], cnt == 0, cnt == 3), ["Vmp", ("pT", pi)], [("ps", ob)])
                        p.op("pe", MM(ps[db][:], onesp[par][:], pT[pi][:], cnt == 0, cnt == 3), [("onesp", par), ("pT", pi)], [("ps", db)])
                        cnt += 1
                p.op("dve", RECIP(tmp[1][:], ps[db][:]), [("ps", db)], [("tmp", 1)])
                p.op("dve", TT(tmp[1][:], ps[ob][:], tmp[1][:], ALU.mult), [("ps", ob), ("tmp", 1)], [("tmp", 1)])
                p.op("dve", TT(yg[:, 16 + c2, g * 512:(g + 1) * 512], tmp[1][:], sgt[:, c2, g * 512:(g + 1) * 512], ALU.mult),
                     [("tmp", 1), "sgt"], [("yg", 16 + c2)])

        ygall = [("yg", c) for c in range(GC)]
        for g in range(NG):
            t0 = HALO + g * 512
            p.dma("sp", xg[:], xT_v[:, :, tb + t0:tb + t0 + 512], [], ["xg"], "xin")
            for m in range(KC):
                bank = m % 4
                for c in range(GC):
                    p.op("pe", MM(ps[bank][:], wo_sb[:, c, m * 128:(m + 1) * 128], yg[:, c, g * 512:(g + 1) * 512], c == 0, c == GC - 1),
                         ["wo_sb"] + ygall, [("ps", bank)])
                p.op("act", ACT(yT[:, m, :], ps[bank][:], AF.Copy), [("ps", bank)], [("yT", m)])
            norm_stats(lambda k: yT[:, k, :], 512, 1.0 / D, 4, [("yT", k) for k in range(KC)])
            for m in range(KC):
                p.op("dve", STT(yT[:, m, :], yT[:, m, :], gpost[:, m:m + 1], rr[:], ALU.mult, ALU.mult),
                     [("yT", m), "gpost", "rr"], [("yT", m)])
                p.op("pool", TT(xg[:, m, :], xg[:, m, :], yT[:, m, :], ALU.add), ["xg", ("yT", m)], ["xg"])
            c0 = pp * TP + g * 512
            p.dma("sp", xoT_v[:, :, c0:c0 + 512], xg[:], ["xg"], ["xo"], "xout")
            if emit_h_next:
                norm_stats(lambda k: xg[:, k, :], 512, 1.0 / D, 5, ["xg"])
                for k in range(KC):
                    p.op("dve", STT(hn[:, k, :], xg[:, k, :], gnext[:, k:k + 1], rr[:], ALU.mult, ALU.mult),
                         ["xg", "gnext", "rr"], ["hn"])
                p.dma("sp", hnT.rearrange("(k p) t -> p k t", p=128)[:, :, c0:c0 + 512], hn[:], ["hn"], ["hno"], "hout")
    fin = ["xo"] + (["hno"] if emit_h_next else [])
    p.op("sp", lambda e: None, fin, [])
    p.emit()
    p.close()
    return nc


def build_diff(S, layer_idx=1):
    NGq = S // 512
    ND = 4 * NGq
    NTL = S // 128
    lam_init = 0.8 - 0.6 * math.exp(-0.3 * layer_idx)
    nc = bass.Bass("TRN2", target_bir_lowering=False)
    dt = nc.dram_tensor
    hT_d = dt("hT", [D, S], BF16, kind="ExternalInput").ap()
    wblk = dt("wblk", [2, 128, KC, 512], F32, kind="ExternalInput").ap()
    tb_d = dt("tbias", [128, 2, ND], F32, kind="ExternalInput").ap()
    qaug_d = dt("qaug", [4, 512], BF16, kind="ExternalInput").ap()
    kaug_d = dt("kaug", [2, 4, S], BF16, kind="ExternalInput").ap()
    mneg_d = dt("mneg", [128, 512], F32, kind="ExternalInput").ap()
    lam_d = dt("lamv", [128, 4, 64], F32, kind="ExternalInput").ap()
    gsub_d = dt("gsub", [128, 1], F32, kind="ExternalInput").ap()
    yo_d = dt("ygT", [256, S], BF16, kind="ExternalOutput").ap()
    hT_v = hT_d.rearrange("(k p) t -> p k t", p=128)

    p = Prog(nc)
    KT = [p.sbuf("KT%d" % c, [68, S], BF16) for c in range(2)]
    V = p.sbuf("V", [128, NTL, 128], BF16)
    hTg = [p.sbuf("hTg%d" % i, [128, KC, 512], BF16) for i in range(2)]
    QTc = [[p.sbuf("QT%d_%d" % (i, c), [68, 512], BF16) for c in range(2)] for i in range(2)]
    wb = [p.sbuf("wb%d" % i, [128, KC, 512], BF16) for i in range(2)]
    PT = [p.sbuf("PT%d" % i, [128, 1024], BF16) for i in range(2)]
    sbm = p.sbuf("sbm", [128, 1024], F32)
    acc = [p.sbuf("acc%d" % c, [128, 512], F32) for c in range(2)]
    sg = p.sbuf("sg", [128, 512], F32)
    rc = [p.sbuf("rc%d" % c, [128, 512], F32) for c in range(2)]
    t0b = p.sbuf("t0b", [128, 512], F32)
    t1b = p.sbuf("t1b", [128, 512], F32)
    ob = p.sbuf("ob", [128, 512], F32)
    sqo = p.sbuf("sqo", [128, 512], BF16)
    rs = p.sbuf("rs", [128, 512], F32)
    yq = [p.sbuf("yq%d" % i, [128, 512], BF16) for i in range(2)]
    tbias = p.sbuf("tbias_s", [128, 2, ND], F32)
    mneg = p.sbuf("mneg_s", [128, 512], F32)
    lamv = p.sbuf("lamv_s", [128, 4, 64], F32)
    lp = p.sbuf("lp", [128, 2, 64], F32)
    ls = p.sbuf("ls", [128, 2], F32)
    nlam = p.sbuf("nlam", [128, 1], F32)
    gsub = p.sbuf("gsub_s", [128, 1], F32)
    ones_b = p.sbuf("ones_b", [128, 128], BF16)
    ones_f = p.sbuf("ones_f", [128, 128], F32)
    psS = [p.psum("psS%d" % i, [128, 1024], F32) for i in range(2)]
    psO = [p.psum("psO%d" % c, [128, 512], F32) for c in range(2)]
    psA = p.psum("psA", [128, 512], F32)
    psB = p.psum("psB", [128, 512], F32)

    p.op("dve", MSET(ones_b[:], 1.0), [], ["ones_b"])
    p.op("dve", MSET(ones_f[:], 1.0), [], ["ones_f"])
    p.dma("sp", tbias[:], tb_d, [], ["tbias"], "c0")
    p.dma("sp", mneg[:], mneg_d, [], ["mneg"], "c1")
    p.dma("sp", lamv[:], lam_d, [], ["lamv"], "c2")
    p.dma("sp", gsub[:], gsub_d, [], ["gsub"], "c3")
    for i in range(2):
        for c in range(2):
            p.dma("sp", QTc[i][c][64:68, :], qaug_d, [], [("QT", i, c)], "c4_%d_%d" % (i, c))
    for j in range(2):
        p.op("dve", TT(lp[:, j, :], lamv[:, 2 * j, :], lamv[:, 2 * j + 1, :], ALU.mult), ["lamv"], ["lp"])
        p.op("dve", lambda e, j=j: e.reduce_sum(ls[:, j:j + 1], lp[:, j, :], axis=mybir.AxisListType.X), ["lp"], ["ls"])
    p.op("act", ACT(ls[:], ls[:], AF.Exp), ["ls"], ["ls"])
    p.op("dve", TT(nlam[:], ls[:, 1:2], ls[:, 0:1], ALU.subtract), ["ls"], ["nlam"])
    p.op("dve", TS(nlam[:], nlam[:], float(-lam_init), ALU.add), ["nlam"], ["nlam"])
    p.op("dve", TS(gsub[:], gsub[:], float(1.0 - lam_init), ALU.mult), ["gsub"], ["gsub"])

    for hh in range(2):
        s = hh
        p.dma("pool", wb[s][:], wblk[hh], [], [("wb", s)], "w%d" % s)
        for c in range(2):
            p.dma("sp", KT[c][64:68, :], kaug_d[hh], [], [("KTaug", c)], "ka%d" % c)
        for g in range(NGq):
            gi = g % 2
            p.dma("sp", hTg[gi][:], hT_v[:, :, g * 512:(g + 1) * 512], [], [("hTg", gi)], "h%d" % gi)
            hk = [("hTg", gi), ("wb", s)]
            for c in range(2):
                for k in range(KC):
                    p.op("pe", MM(psA[:64, :], wb[s][:, k, c * 64:(c + 1) * 64], hTg[gi][:, k, :], k == 0, k == KC - 1), hk, ["psA"])
                p.op("act", ACT(QTc[gi][c][0:64, :], psA[:64, :], AF.Copy, scale=0.125), ["psA"], [("QT", gi, c)])
                for k in range(KC):
                    p.op("pe", MM(psB[:64, :], wb[s][:, k, 128 + c * 64:128 + (c + 1) * 64], hTg[gi][:, k, :], k == 0, k == KC - 1), hk, ["psB"])
                p.op("dve", CP(KT[c][0:64, g * 512:(g + 1) * 512], psB[:64, :]), ["psB"], [("KT", c, g)])
            for t in range(4):
                for k in range(KC):
                    p.op("pe", MM(psA[:, t * 128:(t + 1) * 128], hTg[gi][:, k, t * 128:(t + 1) * 128], wb[s][:, k, 256:384], k == 0, k == KC - 1), hk, ["psA"])
            p.op("dve", CP(V[:, 4 * g:4 * g + 4, :], psA[:].rearrange("p (t e) -> p t e", e=128)), ["psA"], [("V", g)])
            for k in range(KC):
                p.op("pe", MM(psB[:], wb[s][:, k, 384:512], hTg[gi][:, k, :], k == 0, k == KC - 1), hk, ["psB"])
            p.op("act", ACT(sg[:], psB[:], AF.Silu), ["psB"], ["sg"])
            nkb = 4 * g + 4
            for kb in range(nkb):
                j = kb - 4 * g
                q0 = 128 * j if j > 0 else 0
                N = 512 - q0
                si = kb % 2
                delta = 4 * g + 3 - kb
                bias = tbias[:, hh, delta:delta + 1]
                kg = kb // 4
                for c in range(2):
                    p.op("pe", MM(psS[si][:, c * 512 + q0:(c + 1) * 512], KT[c][0:68, kb * 128:(kb + 1) * 128], QTc[gi][c][0:68, q0:512], True, True),
                         [("KT", c, kg), ("KTaug", c), ("QT", gi, c)], [("psS", si, c)])
                if j < 0:
                    p.op("act", ACT(PT[si][:], psS[si][:], AF.Exp, bias=bias), [("psS", si, 0), ("psS", si, 1), "tbias"], [("PT", si, 0), ("PT", si, 1)])
                else:
                    for c in range(2):
                        p.op("dve", TT(sbm[:, c * 512 + q0:(c + 1) * 512], psS[si][:, c * 512 + q0:(c + 1) * 512], mneg[:, 0:N], ALU.add),
                             [("psS", si, c), "mneg"], [("sbm", c)])
                        p.op("act", ACT(PT[si][:, c * 512 + q0:(c + 1) * 512], sbm[:, c * 512 + q0:(c + 1) * 512], AF.Exp, bias=bias),
                             [("sbm", c), "tbias"], [("PT", si, c)])
                for c in range(2):
                    eng = "pool" if c == 0 else "dve"
                    src = PT[si][:, c * 512 + q0:(c + 1) * 512]
                    if kb == 0:
                        p.op(eng, CP(acc[c][:], src), [("PT", si, c)], [("acc", c)])
                    else:
                        p.op(eng, TT(acc[c][:, q0:512], acc[c][:, q0:512], src, ALU.add), [("PT", si, c), ("acc", c)], [("acc", c)])
                    p.op("pe", MM(psO[c][:, q0:512], V[:, kb, :], src, kb == 0, kb == nkb - 1), [("V", kg), ("PT", si, c)], [("psO", c)])
            for c in range(2):
                ps_ = psA if c == 0 else psB
                pk = "psA" if c == 0 else "psB"
                p.op("pe", MM(ps_[:], ones_f[:], acc[c][:], True, True), ["ones_f", ("acc", c)], [pk])
                p.op("dve", RECIP(rc[c][:], ps_[:]), [pk], [("rc", c)])
            p.op("dve", TT(t0b[:], psO[0][:], rc[0][:], ALU.mult), [("psO", 0), ("rc", 0)], ["t0b"])
            p.op("dve", TT(t1b[:], psO[1][:], rc[1][:], ALU.mult), [("psO", 1), ("rc", 1)], ["t1b"])
            p.op("dve", STT(ob[:], t1b[:], nlam[:, 0:1], t0b[:], ALU.mult, ALU.add), ["t0b", "t1b", "nlam"], ["ob"])
            p.op("pool", TT(sqo[:], ob[:], ob[:], ALU.mult), ["ob"], ["sqo"])
            p.op("pe", MM(psA[:], ones_b[:], sqo[:], True, True), ["ones_b", "sqo"], ["psA"])
            p.op("act", ACT(rs[:], psA[:], AF.Ln, bias=EPS, scale=1.0 / 128), ["psA"], ["rs"])
            p.op("act", ACT(rs[:], rs[:], AF.Exp, scale=-0.5), ["rs"], ["rs"])
            p.op("dve", STT(ob[:], ob[:], gsub[:, 0:1], rs[:], ALU.mult, ALU.mult), ["ob", "gsub", "rs"], ["ob"])
            yi = g % 2
            p.op("dve", TT(yq[yi][:], ob[:], sg[:], ALU.mult), ["ob", "sg"], [("yq", yi)])
            p.dma("sp", yo_d[hh * 128:(hh + 1) * 128, g * 512:(g + 1) * 512], yq[yi][:], [("yq", yi)], ["yo"], "yo%d" % yi)
    p.op("sp", lambda e: None, ["yo"], [])
    p.emit()
    p.close()
    return nc


_PROGS = {}
_DBG = {}


def _prog(key, fn):
    if key not in _PROGS:
        _PROGS[key] = fn()
    return _PROGS[key]


def _launch(nc, in_maps):
    res = run_bass_kernel_spmd(nc, in_maps, core_ids=list(range(NCORES)))
    return res.results


def _tok_common(inp, i, memT):
    f = lambda a: np.asarray(a, np.float32)
    return {
        "memT": memT,
        "wkv": arrange_blocks(f(inp["w_mem_kv_%d" % i]), memkv_colsets()),
        "wo": arrange_wout(f(inp["w_out_%d" % i])),
        "gpre": arrange_vec(inp["norm_pre_%d" % i], KC),
        "gpost": arrange_vec(inp["norm_post_%d" % i], KC),
        "gmem": arrange_vec(inp["norm_mem_%d" % i], KC),
    }


def _halo_split(xT_full, TC, HALO):
    S = xT_full.shape[1]
    outs = []
    for c in range(NCORES):
        own = xT_full[:, c * TC:(c + 1) * TC]
        if HALO == 0:
            outs.append(np.ascontiguousarray(own))
            continue
        if c == 0:
            h = np.zeros((D, HALO), np.float32)
        else:
            h = xT_full[:, c * TC - HALO:c * TC]
        outs.append(np.ascontiguousarray(np.concatenate([h, own], axis=1)))
    return outs


def run_model(inp, S, stop_after=None):
    f = lambda a: np.asarray(a, np.float32)
    TC = S // NCORES
    TPc = min(1024, TC)
    TPs = min(512, TC)
    x = f(inp["x"])[0]
    xT_full = np.ascontiguousarray(x.T)
    memT = np.ascontiguousarray(f(inp["mem"])[0].T)

    def conv_layer(i, xT_full, emit_next):
        nc = _prog(("conv", TC, TPc, emit_next), lambda: build_tok("conv", TC, TPc, emit_h_next=emit_next))
        com = _tok_common(inp, i, memT)
        com["wblk"] = arrange_blocks(f(inp["w_in_%d" % i]), conv_colsets())
        cwv = f(inp["conv_w_%d" % i])
        com["cw"] = np.ascontiguousarray(cwv.reshape(3, 16, 128).transpose(2, 1, 0))
        if emit_next:
            com["gnext"] = arrange_vec(inp["norm_pre_%d" % (i + 1)], KC)
        xs = _halo_split(xT_full, TC, 16)
        res = _launch(nc, [dict(com, xT=xs[c]) for c in range(NCORES)])
        xo = np.concatenate([r["xoT"] for r in res], axis=1)
        hn = np.concatenate([np.asarray(r["hnT"]) for r in res], axis=1) if emit_next else None
        return xo, hn

    x1T, h1T = conv_layer(0, xT_full, True)
    if stop_after == 0:
        return x1T, h1T

    i = 1
    nc = _prog(("diff", S), lambda: build_diff(S, 1))
    slopes = alibi_slopes(16).astype(np.float64)
    NGq = S // 512
    ND = 4 * NGq
    w1 = f(inp["w_in_1"])
    ql = np.arange(512)
    qaug = bf(np.stack([ql // 16, ql % 16, ql // 16, ql % 16]).astype(np.float32))
    mneg = np.zeros((128, 512), np.float32)
    mneg[:, :128] = np.where(np.arange(128)[None, :] >= np.arange(128)[:, None], 0.0, NEG)
    lamv = np.stack([f(inp["lambda_q1_1"]), f(inp["lambda_k1_1"]), f(inp["lambda_q2_1"]), f(inp["lambda_k2_1"])])
    lamv = np.ascontiguousarray(np.broadcast_to(lamv[None], (128, 4, 64)))
    gsub = np.ascontiguousarray(f(inp["subln_1"]).reshape(128, 1))
    h1T = np.ascontiguousarray(h1T)
    maps = []
    for c in range(NCORES):
        colsets = []
        tb = np.zeros((128, 2, ND), np.float32)
        kaug = np.zeros((2, 4, S), np.float32)
        for hh in range(2):
            h = 2 * c + hh
            cs = []
            for base in (0, 2048):
                for cm in range(2):
                    cs += list(range(base + h * 128 + cm * 64, base + h * 128 + cm * 64 + 64))
            cs += list(range(4096 + h * 128, 4096 + h * 128 + 128))
            cs += list(range(6400 + h * 128, 6400 + h * 128 + 128))
            colsets.append(cs)
            sl = float(slopes[h])
            dl = np.arange(ND)[None, :]
            tb[:, hh, :] = (sl * (128.0 * (3 - dl) + np.arange(128)[:, None])).astype(np.float32)
            s_hi = float(np.float32(sl).astype(ml_dtypes.bfloat16))
            s_lo = float(np.float32(sl - s_hi).astype(ml_dtypes.bfloat16))
            kaug[hh] = np.array([-16 * s_hi, -s_hi, -16 * s_lo, -s_lo], np.float32)[:, None]
        maps.append({"hT": h1T, "wblk": arrange_blocks(w1, colsets), "tbias": tb, "qaug": qaug,
                     "kaug": bf(kaug), "mneg": mneg, "lamv": lamv, "gsub": gsub})
    res = _launch(nc, maps)
    mixT_full = np.concatenate([np.asarray(r["ygT"]) for r in res], axis=0)
    if stop_after == "1a":
        return mixT_full

    nc = _prog(("diffpost", TC, TPc), lambda: build_tok("diffpost", TC, TPc))
    com = _tok_common(inp, 1, memT)
    com["wblk"] = arrange_blocks(w1, diffpost_colsets())
    res = _launch(nc, [dict(com, xT=np.ascontiguousarray(x1T[:, c * TC:(c + 1) * TC]),
                            mixT=np.ascontiguousarray(mixT_full[:, c * TC:(c + 1) * TC])) for c in range(NCORES)])
    x2T = np.concatenate([r["xoT"] for r in res], axis=1)
    _DBG["x2T"] = x2T
    if stop_after == 1:
        return x2T

    nc = _prog(("swa", TC, TPs), lambda: build_tok("swa", TC, TPs))
    com = _tok_common(inp, 2, memT)
    com["wblk"] = arrange_blocks(f(inp["w_in_2"]), swa_colsets())
    sidx = np.arange(128)[:, None]
    qidx = np.arange(128)[None, :]
    rel = np.concatenate([qidx - sidx + 128, qidx - sidx], axis=1).astype(np.float32)
    com["rel2"] = rel
    com["mneg2"] = np.where((rel >= 0) & (rel < 128), 0.0, NEG).astype(np.float32)
    sk = f(inp["sinks_2"])
    com["sinks"] = np.ascontiguousarray(sk.reshape(16, 2)[:, (np.arange(128) // 64)].T)
    xs = _halo_split(x2T, TC, 128)
    maps = []
    for c in range(NCORES):
        hv = np.full((128, 1), NEG if c == 0 else 0.0, np.float32)
        maps.append(dict(com, xT=xs[c], hvneg=hv))
    res = _launch(nc, maps)
    x3T = np.concatenate([r["xoT"] for r in res], axis=1)
    _DBG["x3T"] = x3T
    if stop_after == 2:
        return x3T

    x4T, _ = conv_layer(3, x3T, False)
    return x4T


_INPUT_NAMES = [
    "x", "mem", "positions",
    "norm_pre_0", "norm_post_0", "norm_mem_0", "w_in_0", "w_mem_kv_0", "conv_w_0", "w_out_0",
    "norm_pre_1", "norm_post_1", "norm_mem_1", "w_in_1", "w_mem_kv_1",
    "lambda_q1_1", "lambda_k1_1", "lambda_q2_1", "lambda_k2_1", "subln_1", "w_out_1",
    "norm_pre_2", "norm_post_2", "norm_mem_2", "w_in_2", "w_mem_kv_2", "sinks_2", "w_out_2",
    "norm_pre_3", "norm_post_3", "norm_mem_3", "w_in_3", "w_mem_kv_3", "conv_w_3", "w_out_3",
]


def kernel(**inputs):
    inputs = {n: inputs[n] for n in _INPUT_NAMES}
    S = inputs["x"].shape[1]
    outT = run_model(inputs, S)
    return np.ascontiguousarray(outT.T)[None].astype(np.float32)
```

```python
import contextlib
import math
import numpy as np
import ml_dtypes
import concourse.bass as bass
import concourse.mybir as mybir
from concourse.bass_utils import run_bass_kernel_spmd

F32 = mybir.dt.float32
BF16 = mybir.dt.bfloat16
AF = mybir.ActivationFunctionType
ALU = mybir.AluOpType

NCORES = 8
D = 1024
KC = 8
BR = 2048
GW = 2304
GC = 18
EPS = 1e-6
NEG = -1e30
SAME_ENGINE_SYNC = True


class _Op:
    __slots__ = ("eng", "fn", "reads", "writes", "stream", "deps", "signal", "sig_sem", "sig_val")

    def __init__(self, eng, fn, reads, writes, stream):
        self.eng = eng
        self.fn = fn
        self.reads = tuple(reads)
        self.writes = tuple(writes)
        self.stream = stream
        self.deps = ()
        self.signal = False
        self.sig_sem = None
        self.sig_val = 0


class Prog:
    ENGS = ("pe", "act", "dve", "pool", "sp")

    def __init__(self, nc):
        self.nc = nc
        self.ops = []
        self.stack = contextlib.ExitStack()

    def sbuf(self, name, shape, dtype):
        return self.stack.enter_context(self.nc.sbuf_tensor(name, list(shape), dtype))

    def psum(self, name, shape, dtype):
        return self.stack.enter_context(self.nc.psum_tensor(name, list(shape), dtype))

    def op(self, eng, fn, reads=(), writes=(), stream=None):
        o = _Op(eng, fn, reads, writes, stream)
        self.ops.append(o)
        return o

    def dma(self, q, out, in_, reads, writes, stream):
        return self.op(q, lambda e: e.dma_start(out=out, in_=in_), reads, writes, stream)

    def emit(self):
        nc = self.nc
        ops = self.ops
        last_writer = {}
        readers = {}
        for i, o in enumerate(ops):
            deps = set()
            for b in o.reads:
                w = last_writer.get(b)
                if w is not None:
                    deps.add(w)
            for b in o.writes:
                w = last_writer.get(b)
                if w is not None:
                    deps.add(w)
                r = readers.get(b)
                if r:
                    deps.update(r)
            deps.discard(i)
            o.deps = deps
            for b in o.reads:
                readers.setdefault(b, []).append(i)
            for b in o.writes:
                last_writer[b] = i
                readers[b] = []
        for o in ops:
            for d in o.deps:
                od = ops[d]
                if od.stream is not None or od.eng != o.eng or (SAME_ENGINE_SYNC and od.eng != "pe"):
                    od.signal = True
        for o in ops:
            if o.stream is not None:
                o.signal = True
        sems = {}
        counts = {}
        for o in ops:
            if not o.signal:
                continue
            key = ("d", o.stream) if o.stream is not None else ("e", o.eng)
            if key not in sems:
                sems[key] = self.stack.enter_context(nc.semaphore("s_%s_%s" % key))
                counts[key] = 0
            counts[key] += 16 if o.stream is not None else 1
            o.sig_sem = key
            o.sig_val = counts[key]
        self.sem_counts = dict(counts)
        per_eng = {e: [o for o in ops if o.eng == e] for e in self.ENGS}
        waited = {}

        def run(engname, e):
            for o in per_eng[engname]:
                need = {}
                for d in o.deps:
                    od = ops[d]
                    if not od.signal:
                        continue
                    if od.stream is None and od.eng == engname and not (SAME_ENGINE_SYNC and engname != "pe"):
                        continue
                    k = od.sig_sem
                    if od.sig_val > need.get(k, 0):
                        need[k] = od.sig_val
                for k, v in need.items():
                    if waited.get((engname, k), 0) >= v:
                        continue
                    waited[(engname, k)] = v
                    e.wait_ge(sems[k], v)
                ins = o.fn(e)
                if o.signal:
                    ins.then_inc(sems[o.sig_sem], 16 if o.stream is not None else 1)

        with nc.Block() as block:
            @block.tensor
            def _(e):
                run("pe", e)

            @block.scalar
            def _(e):
                run("act", e)

            @block.vector
            def _(e):
                run("dve", e)

            @block.gpsimd
            def _(e):
                run("pool", e)

            @block.sync
            def _(e):
                run("sp", e)

    def close(self):
        self.stack.close()


def MM(out, lhsT, rhs, start, stop):
    return lambda e: e.matmul(out, lhsT, rhs, start=start, stop=stop)


def ACT(out, in_, func, bias=None, scale=None):
    kw = {}
    if bias is not None:
        kw["bias"] = bias
    if scale is not None:
        kw["scale"] = scale
    return lambda e: e.activation(out, in_, func, **kw)


def TT(out, a, b, op):
    return lambda e: e.tensor_tensor(out, a, b, op)


def TS(out, a, s1, op0, s2=None, op1=None):
    if op1 is None:
        return lambda e: e.tensor_scalar(out, a, s1, None, op0)
    return lambda e: e.tensor_scalar(out, a, s1, s2, op0, op1)


def STT(out, in0, scalar, in1, op0, op1):
    return lambda e: e.scalar_tensor_tensor(out, in0, scalar, in1, op0, op1)


def CP(out, in_):
    return lambda e: e.tensor_copy(out, in_)


def MSET(out, v):
    return lambda e: e.memset(out, v)


def RECIP(out, in_):
    return lambda e: e.reciprocal(out, in_)


def alibi_slopes(n):
    return (2.0 ** (-8.0 * np.arange(1, n + 1, dtype=np.float64) / n)).astype(np.float32)


def bf(x):
    return np.ascontiguousarray(np.asarray(x).astype(ml_dtypes.bfloat16))


def arrange_blocks(W, colsets):
    nb = len(colsets)
    out = np.zeros((nb, 128, KC, 512), np.float32)
    Wr = W.reshape(KC, 128, -1)
    for b, cs in enumerate(colsets):
        cs = np.asarray(cs)
        ok = cs >= 0
        blk = np.zeros((KC, 128, 512), np.float32)
        blk[:, :, ok] = Wr[:, :, cs[ok]]
        out[b] = blk.transpose(1, 0, 2)
    return out


def arrange_wout(Wo):
    return np.ascontiguousarray(Wo.reshape(GC, 128, D).transpose(1, 0, 2))


def arrange_vec(v, n):
    return np.ascontiguousarray(np.asarray(v, np.float32).reshape(n, 128).T)


def tail_cols(q0, g0):
    return list(range(q0, q0 + 256)) + list(range(g0 + 2048, g0 + 2304))


def conv_colsets():
    cs = []
    for fc in range(16):
        c = []
        for base in (0, 2048, 4096, 6400):
            c += list(range(base + fc * 128, base + fc * 128 + 128))
        cs.append(c)
    cs.append(tail_cols(6144, 6400))
    return cs


def swa_colsets():
    cs = []
    for kb in range(2):
        c = []
        for kvh in (2 * kb, 2 * kb + 1):
            for par in range(2):
                blk = [-1] * 128
                for d in range(64):
                    blk[par * 64 + d] = 2048 + kvh * 64 + d
                c += blk
        cs.append(c)
    cs.append(list(range(2304, 2560)) + [-1] * 256)
    for b in range(8):
        c = []
        for i in (2 * b, 2 * b + 1):
            c += list(range(i * 128, i * 128 + 128))
            c += list(range(2816 + i * 128, 2816 + i * 128 + 128))
        cs.append(c)
    cs.append(tail_cols(2560, 2816))
    return cs


def diffpost_colsets():
    return [tail_cols(6144, 6400)]


def memkv_colsets():
    cs = []
    c = []
    for hm in range(4):
        blk = [-1] * 128
        for d in range(64):
            blk[(hm % 2) * 64 + d] = hm * 64 + d
        c += blk
    cs.append(c)
    cs.append(list(range(256, 512)) + [-1] * 256)
    return cs


def build_tok(kind, TC, TP, emit_h_next=False):
    HALO = {"conv": 16, "swa": 128, "diffpost": 0}[kind]
    NP = TC // TP
    NG = TP // 512
    nblk = {"conv": 17, "swa": 12, "diffpost": 1}[kind]
    nc = bass.Bass("TRN2", target_bir_lowering=False)
    dt = nc.dram_tensor
    xT = dt("xT", [D, HALO + TC], F32, kind="ExternalInput").ap()
    memT = dt("memT", [D, 256], F32, kind="ExternalInput").ap()
    wblk = dt("wblk", [nblk, 128, KC, 512], F32, kind="ExternalInput").ap()
    wkv = dt("wkv", [2, 128, KC, 512], F32, kind="ExternalInput").ap()
    wo = dt("wo", [128, GC, D], F32, kind="ExternalInput").ap()
    gpre_d = dt("gpre", [128, KC], F32, kind="ExternalInput").ap()
    gpost_d = dt("gpost", [128, KC], F32, kind="ExternalInput").ap()
    gmem_d = dt("gmem", [128, KC], F32, kind="ExternalInput").ap()
    if emit_h_next:
        gnext_d = dt("gnext", [128, KC], F32, kind="ExternalInput").ap()
        hnT = dt("hnT", [D, TC], BF16, kind="ExternalOutput").ap()
    if kind == "conv":
        cw_d = dt("cw", [128, 16, 3], F32, kind="ExternalInput").ap()
    if kind == "swa":
        rel_d = dt("rel2", [128, 256], F32, kind="ExternalInput").ap()
        mneg_d = dt("mneg2", [128, 256], F32, kind="ExternalInput").ap()
        hv_d = dt("hvneg", [128, 1], F32, kind="ExternalInput").ap()
        sk_d = dt("sinks", [128, 16], F32, kind="ExternalInput").ap()
    if kind == "diffpost":
        mixT = dt("mixT", [BR, TC], BF16, kind="ExternalInput").ap()
    xoT = dt("xoT", [D, TC], F32, kind="ExternalOutput").ap()

    xT_v = xT.rearrange("(k p) t -> p k t", p=128)
    xoT_v = xoT.rearrange("(k p) t -> p k t", p=128)
    memT_v = memT.rearrange("(k p) t -> p k t", p=128)

    p = Prog(nc)
    NT = HALO + TP
    hT = p.sbuf("hT", [128, KC, NT], BF16)
    yg = p.sbuf("yg", [128, GC, TP], BF16)
    wo_sb = p.sbuf("wo_sb", [128, GC, D], BF16)
    xg = p.sbuf("xg", [128, KC, 512], F32)
    yT = p.sbuf("yT", [128, KC, 512], F32)
    NW = 3
    wb = [p.sbuf("wb%d" % i, [128, KC, 512], BF16) for i in range(NW)]
    sq = [p.sbuf("sq%d" % i, [128, 512], BF16) for i in range(2)]
    rr = p.sbuf("rr", [128, 512], F32)
    tmp = [p.sbuf("tmp%d" % i, [128, 512], F32) for i in range(2)]
    ones_b = p.sbuf("ones_b", [128, 128], BF16)
    onesp = [p.sbuf("onesp%d" % i, [128, 128], BF16) for i in range(2)]
    gpre = p.sbuf("gpre_s", [128, KC], F32)
    gpost = p.sbuf("gpost_s", [128, KC], F32)
    gmem = p.sbuf("gmem_s", [128, KC], F32)
    mnT = p.sbuf("mnT", [128, KC, 256], BF16)
    KmT = p.sbuf("KmT", [128, 4, 256], BF16)
    Vmp = p.sbuf("Vmp", [128, 4, 2, 128], BF16)
    qmT = p.sbuf("qmT", [128, 2, TP], BF16)
    sgt = p.sbuf("sgt", [128, 2, TP], BF16)
    pT = [p.sbuf("pT%d" % i, [128, 512], BF16) for i in range(2)]
    if emit_h_next:
        gnext = p.sbuf("gnext_s", [128, KC], F32)
        hn = p.sbuf("hn", [128, KC, 512], BF16)
    if kind == "conv":
        cw = p.sbuf("cw_s", [128, 16, 3], F32)
        zb = p.sbuf("zb", [128, HALO + TP], F32)
        u_sb = p.sbuf("u_sb", [128, 512], F32)
        sg = p.sbuf("sg", [128, 512], F32)
        cb = p.sbuf("cb", [128, 512], F32)
    if kind == "swa":
        rel2 = p.sbuf("rel2_s", [128, 256], F32)
        mneg2 = p.sbuf("mneg2_s", [128, 256], F32)
        hvneg = p.sbuf("hv_s", [128, 1], F32)
        sinks = p.sbuf("sinks_s", [128, 16], F32)
        esink = p.sbuf("esink", [128, 16], F32)
        NTL = NT // 128
        KTp = p.sbuf("KTp", [128, 8, NT], BF16)
        Vp = p.sbuf("Vp", [128, 8, NTL, 128], BF16)
        QT = p.sbuf("QT", [128, TP], BF16)
        sgq = p.sbuf("sgq", [128, TP], F32)
        b4 = p.sbuf("b4", [128, 512], F32)
        b4f = p.sbuf("b4f", [128, 512], F32)
        sb4 = p.sbuf("sb4", [128, 512], F32)
        rden = p.sbuf("rden", [128, 512], F32)
    ps = [p.psum("ps%d" % i, [128, 512], F32) for i in range(8)]

    p.op("dve", MSET(ones_b[:], 1.0), [], ["ones_b"])
    for par in range(2):
        p.op("dve", MSET(onesp[par][:], 0.0), [], [("onesp", par)])
        p.op("dve", MSET(onesp[par][:, par * 64:(par + 1) * 64], 1.0), [], [("onesp", par)])
    p.dma("sp", gpre[:], gpre_d, [], ["gpre"], "c0")
    p.dma("sp", gpost[:], gpost_d, [], ["gpost"], "c1")
    p.dma("sp", gmem[:], gmem_d, [], ["gmem"], "c2")
    if emit_h_next:
        p.dma("sp", gnext[:], gnext_d, [], ["gnext"], "c3")
    if kind == "conv":
        p.dma("sp", cw[:], cw_d, [], ["cw"], "c4")
    if kind == "swa":
        p.dma("sp", rel2[:], rel_d, [], ["rel2"], "c4")
        p.dma("sp", mneg2[:], mneg_d, [], ["mneg2"], "c5")
        p.dma("sp", hvneg[:], hv_d, [], ["hvneg"], "c6")
        p.dma("sp", sinks[:], sk_d, [], ["sinks"], "c7")
        p.op("act", ACT(esink[:], sinks[:], AF.Exp), ["sinks"], ["esink"])
        p.op("pool", MSET(Vp[:], 0.0), [], ["Vp"])
    p.op("pool", MSET(Vmp[:], 0.0), [], ["Vmp"])

    wstate = {"n": 0}

    def load_w(src_ap):
        s = wstate["n"] % NW
        wstate["n"] += 1
        p.dma("pool", wb[s][:], src_ap, [], [("wb", s)], "w%d" % s)
        return s

    sqi = {"n": 0}

    def norm_stats(src_of_k, N, nparts_scale, bank, tagreads):
        for k in range(KC):
            s = sqi["n"] % 2
            sqi["n"] += 1
            src = src_of_k(k)
            p.op("pool", TT(sq[s][:, :N], src, src, ALU.mult), tagreads, [("sq", s)])
            p.op("pe", MM(ps[bank][:, :N], ones_b[:], sq[s][:, :N], k == 0, k == KC - 1),
                 ["ones_b", ("sq", s)], [("ps", bank)])
        p.op("act", ACT(rr[:, :N], ps[bank][:, :N], AF.Ln, bias=EPS, scale=nparts_scale), [("ps", bank)], ["rr"])
        p.op("act", ACT(rr[:, :N], rr[:, :N], AF.Exp, scale=-0.5), ["rr"], ["rr"])

    def proj(bank, slot, c0, M, t0, N, pkey=None):
        for k in range(KC):
            p.op("pe", MM(ps[bank][:M, :N], wb[slot][:, k, c0:c0 + M], hT[:, k, t0:t0 + N], k == 0, k == KC - 1),
                 [("wb", slot), "hT"], [("ps", bank)])

    p.dma("sp", xg[:, :, 0:256], memT_v, [], ["xg"], "xin")
    norm_stats(lambda k: xg[:, k, 0:256], 256, 1.0 / D, 0, ["xg"])
    for k in range(KC):
        p.op("dve", STT(mnT[:, k, :], xg[:, k, 0:256], gmem[:, k:k + 1], rr[:, 0:256], ALU.mult, ALU.mult),
             ["xg", "gmem", "rr"], ["mnT"])
    sk = load_w(wkv[0])
    sv = load_w(wkv[1])
    for hm in range(4):
        bank = 1 + (hm % 2)
        for k in range(KC):
            p.op("pe", MM(ps[bank][:, :256], wb[sk][:, k, hm * 128:(hm + 1) * 128], mnT[:, k, :], k == 0, k == KC - 1),
                 [("wb", sk), "mnT"], [("ps", bank)])
        p.op("act", ACT(KmT[:, hm, :], ps[bank][:, :256], AF.Copy, scale=0.125), [("ps", bank)], ["KmT"])
    for t in range(2):
        bank = 3 + t
        for k in range(KC):
            p.op("pe", MM(ps[bank][:, :256], mnT[:, k, t * 128:(t + 1) * 128], wb[sv][:, k, 0:256], k == 0, k == KC - 1),
                 [("wb", sv), "mnT"], [("ps", bank)])
        for hm in range(4):
            par = hm % 2
            p.op("dve", CP(Vmp[:, hm, t, par * 64:(par + 1) * 64], ps[bank][:, hm * 64:(hm + 1) * 64]),
                 [("ps", bank)], ["Vmp"])

    for pp in range(NP):
        tb = pp * TP
        if pp == 0:
            for c3 in range(3):
                p.dma("pool", wo_sb[:, c3 * 6:(c3 + 1) * 6, :], wo[:, c3 * 6:(c3 + 1) * 6, :], [], ["wo_sb"], "wo")
        groups = []
        if HALO:
            groups.append((0, HALO))
        for g in range(NG):
            groups.append((HALO + g * 512, 512))
        for (t0, N) in groups:
            p.dma("sp", xg[:, :, 0:N], xT_v[:, :, tb + t0:tb + t0 + N], [], ["xg"], "xin")
            norm_stats(lambda k, N=N: xg[:, k, 0:N], N, 1.0 / D, 0, ["xg"])
            for k in range(KC):
                p.op("dve", STT(hT[:, k, t0:t0 + N], xg[:, k, 0:N], gpre[:, k:k + 1], rr[:, 0:N], ALU.mult, ALU.mult),
                     ["xg", "gpre", "rr"], ["hT"])

        if kind == "conv":
            for fc in range(16):
                s = load_w(wblk[fc])
                for (sec, bank) in ((1, 0), (2, 1)):
                    proj(bank, s, sec * 128, 128, 0, HALO)
                p.op("act", ACT(u_sb[:, :HALO], ps[1][:, :HALO], AF.Copy), [("ps", 1)], ["u_sb"])
                p.op("dve", TT(zb[:, 0:HALO], ps[0][:, :HALO], u_sb[:, :HALO], ALU.mult), [("ps", 0), "u_sb"], [("zb", -1)])
                for g in range(NG):
                    t0 = HALO + g * 512
                    bb = 4 * (g % 2)
                    for sec in range(4):
                        proj(bb + sec, s, sec * 128, 128, t0, 512)
                    p.op("act", ACT(u_sb[:], ps[bb + 2][:], AF.Copy), [("ps", bb + 2)], ["u_sb"])
                    p.op("act", ACT(sg[:], ps[bb + 3][:], AF.Silu), [("ps", bb + 3)], ["sg"])
                    p.op("dve", TT(zb[:, t0:t0 + 512], ps[bb + 1][:], u_sb[:], ALU.mult), [("ps", bb + 1), "u_sb"], [("zb", g)])
                    zr = [("zb", g - 1), ("zb", g)]
                    p.op("pool", TS(cb[:], zb[:, t0 - 2:t0 + 510], cw[:, fc, 0:1], ALU.mult), zr + ["cw"], ["cb"])
                    p.op("dve", STT(cb[:], zb[:, t0 - 1:t0 + 511], cw[:, fc, 1:2], cb[:], ALU.mult, ALU.add), zr + ["cw", "cb"], ["cb"])
                    p.op("dve", STT(cb[:], zb[:, t0:t0 + 512], cw[:, fc, 2:3], cb[:], ALU.mult, ALU.add), zr + ["cw", "cb"], ["cb"])
                    p.op("dve", TT(tmp[0][:], ps[bb][:], cb[:], ALU.mult), [("ps", bb), "cb"], [("tmp", 0)])
                    p.op("dve", TT(yg[:, fc, g * 512:(g + 1) * 512], tmp[0][:], sg[:], ALU.mult), [("tmp", 0), "sg"], [("yg", fc)])
        elif kind == "diffpost":
            mix_v = mixT.rearrange("(c p) t -> p c t", p=128)
            for c4 in range(4):
                p.dma("sp", yg[:, c4 * 4:(c4 + 1) * 4, :], mix_v[:, c4 * 4:(c4 + 1) * 4, tb:tb + TP], [],
                      [("yg", c4 * 4 + i) for i in range(4)], "mixin%d" % c4)
        elif kind == "swa":
            slopes = alibi_slopes(32)
            for kb in range(2):
                s = load_w(wblk[kb])
                for q in range(4):
                    idx = kb * 4 + q
                    for (t0, N) in groups:
                        bank = q % 4
                        proj(bank, s, q * 128, 128, t0, N)
                        p.op("act", ACT(KTp[:, idx, t0:t0 + N], ps[bank][:, :N], AF.Copy, scale=0.125), [("ps", bank)], ["KTp"])
            s = load_w(wblk[2])
            for tl in range(NTL):
                bank = 4 + (tl % 2)
                for k in range(KC):
                    p.op("pe", MM(ps[bank][:, :256], hT[:, k, tl * 128:(tl + 1) * 128], wb[s][:, k, 0:256], k == 0, k == KC - 1),
                         [("wb", s), "hT"], [("ps", bank)])
                for kvh in range(4):
                    for par in range(2):
                        eng = "dve" if par == 0 else "act"
                        fn = CP(Vp[:, kvh * 2 + par, tl, par * 64:(par + 1) * 64], ps[bank][:, kvh * 64:(kvh + 1) * 64]) if par == 0 else \
                            ACT(Vp[:, kvh * 2 + par, tl, par * 64:(par + 1) * 64], ps[bank][:, kvh * 64:(kvh + 1) * 64], AF.Copy)
                        p.op(eng, fn, [("ps", bank)], ["Vp"])
            for b in range(8):
                s = load_w(wblk[3 + b])
                for ii in range(2):
                    i = 2 * b + ii
                    kvh = i // 4
                    for g in range(NG):
                        t0 = HALO + g * 512
                        proj(0, s, ii * 256, 128, t0, 512)
                        p.op("act", ACT(QT[:, g * 512:(g + 1) * 512], ps[0][:], AF.Copy), [("ps", 0)], ["QT"])
                        proj(1, s, ii * 256 + 128, 128, t0, 512)
                        p.op("act", ACT(sgq[:, g * 512:(g + 1) * 512], ps[1][:], AF.Silu), [("ps", 1)], ["sgq"])
                    for par in range(2):
                        p.op("dve", STT(b4[:, par * 256:(par + 1) * 256], rel2[:], float(-slopes[2 * i + par]), mneg2[:], ALU.mult, ALU.add),
                             ["rel2", "mneg2"], ["b4"])
                    if pp == 0:
                        p.op("dve", CP(b4f[:], b4[:]), ["b4"], ["b4f"])
                        for par in range(2):
                            p.op("dve", TS(b4f[:, par * 256:par * 256 + 128], b4f[:, par * 256:par * 256 + 128], hvneg[:, 0:1], ALU.add),
                                 ["b4f", "hvneg"], ["b4f"])
                    for n in range(TP // 128):
                        tl = n + 1
                        sbank = 2 + (n % 2)
                        for par in range(2):
                            for j in range(2):
                                p.op("pe", MM(ps[sbank][:, (par * 2 + j) * 128:(par * 2 + j + 1) * 128],
                                              KTp[:, kvh * 2 + par, (tl - 1 + j) * 128:(tl + j) * 128],
                                              QT[:, n * 128:(n + 1) * 128], True, True),
                                     ["KTp", "QT"], [("ps", sbank)])
                        bsrc = b4f if (pp == 0 and n == 0) else b4
                        bkey = "b4f" if (pp == 0 and n == 0) else "b4"
                        p.op("dve", TT(sb4[:], ps[sbank][:], bsrc[:], ALU.add), [("ps", sbank), bkey], ["sb4"])
                        pi = n % 2
                        p.op("act", ACT(pT[pi][:], sb4[:], AF.Exp), ["sb4"], [("pT", pi)])
                        q4 = n % 4
                        ob, db = 4 + ((n // 4) % 2) * 2, 5 + ((n // 4) % 2) * 2
                        cnt = 0
                        for par in range(2):
                            for j in range(2):
                                p.op("pe", MM(ps[ob][:, q4 * 128:(q4 + 1) * 128], Vp[:, kvh * 2 + par, tl - 1 + j, :],
                                              pT[pi][:, (par * 2 + j) * 128:(par * 2 + j + 1) * 128], cnt == 0, cnt == 3),
                                     ["Vp", ("pT", pi)], [("ps", ob)])
                                cnt += 1
                        cnt = 0
                        for par in range(2):
                            for j in range(2):
                                p.op("pe", MM(ps[db][:, q4 * 128:(q4 + 1) * 128], onesp[par][:],
                                              pT[pi][:, (par * 2 + j) * 128:(par * 2 + j + 1) * 128], cnt == 0, cnt == 3),
                                     [("onesp", par), ("pT", pi)], [("ps", db)])
                                cnt += 1
                        if q4 == 3:
                            g = n // 4
                            p.op("dve", TS(rden[:], ps[db][:], esink[:, i:i + 1], ALU.add), [("ps", db), "esink"], ["rden"])
                            p.op("dve", RECIP(rden[:], rden[:]), ["rden"], ["rden"])
                            p.op("dve", TT(tmp[0][:], ps[ob][:], rden[:], ALU.mult), [("ps", ob), "rden"], [("tmp", 0)])
                            p.op("dve", TT(yg[:, i, g * 512:(g + 1) * 512], tmp[0][:], sgq[:, g * 512:(g + 1) * 512], ALU.mult),
                                 [("tmp", 0), "sgq"], [("yg", i)])

        s = load_w(wblk[nblk - 1])
        for c2 in range(2):
            for g in range(NG):
                t0 = HALO + g * 512
                proj(0 + (g % 2) * 2, s, c2 * 128, 128, t0, 512)
                p.op("act", ACT(qmT[:, c2, g * 512:(g + 1) * 512], ps[0 + (g % 2) * 2][:], AF.Copy), [("ps", 0 + (g % 2) * 2)], ["qmT"])
                proj(1 + (g % 2) * 2, s, 256 + c2 * 128, 128, t0, 512)
                p.op("act", ACT(sgt[:, c2, g * 512:(g + 1) * 512], ps[1 + (g % 2) * 2][:], AF.Silu), [("ps", 1 + (g % 2) * 2)], ["sgt"])
        it = 0
        for c2 in range(2):
            for g in range(NG):
                ob, db = 6, 7
                cnt = 0
                for par in range(2):
                    hm = 2 * c2 + par
                    for j in range(2):
                        sbank = 4 + (it % 2)
                        pi = it % 2
                        it += 1
                        p.op("pe", MM(ps[sbank][:], KmT[:, hm, j * 128:(j + 1) * 128], qmT[:, c2, g * 512:(g + 1) * 512], True, True),
                             ["KmT", "qmT"], [("ps", sbank)])
                        p.op("act", ACT(pT[pi][:], ps[sbank][:], AF.Exp), [("ps", sbank)], [("pT", pi)])
                        p.op("pe", MM(ps[ob][:], Vmp[:, hm, j, :], pT[pi][:], cnt == 0, cnt == 3), ["Vmp", ("pT", pi)], [("ps", ob)])
                        p.op("pe", MM(ps[db][:], onesp[par][:], pT[pi][:], cnt == 0, cnt == 3), [("onesp", par), ("pT", pi)], [("ps", db)])
                        cnt += 1
                p.op("dve", RECIP(tmp[1][:], ps[db][:]), [("ps", db)], [("tmp", 1)])
                p.op("dve", TT(tmp[1][:], ps[ob][:], tmp[1][:], ALU.mult), [("ps", ob), ("tmp", 1)], [("tmp", 1)])
                p.op("dve", TT(yg[:, 16 + c2, g * 512:(g + 1) * 512], tmp[1][:], sgt[:, c2, g * 512:(g + 1) * 512], ALU.mult),
                     [("tmp", 1), "sgt"], [("yg", 16 + c2)])

        ygall = [("yg", c) for c in range(GC)]
        for g in range(NG):
            t0 = HALO + g * 512
            p.dma("sp", xg[:], xT_v[:, :, tb + t0:tb + t0 + 512], [], ["xg"], "xin")
            for m in range(KC):
                bank = m % 4
                for c in range(GC):
                    p.op("pe", MM(ps[bank][:], wo_sb[:, c, m * 128:(m + 1) * 128], yg[:, c, g * 512:(g + 1) * 512], c == 0, c == GC - 1),
                         ["wo_sb"] + ygall, [("ps", bank)])
                p.op("act", ACT(yT[:, m, :], ps[bank][:], AF.Copy), [("ps", bank)], [("yT", m)])
            norm_stats(lambda k: yT[:, k, :], 512, 1.0 / D, 4, [("yT", k) for k in range(KC)])
            for m in range(KC):
                p.op("dve", STT(yT[:, m, :], yT[:, m, :], gpost[:, m:m + 1], rr[:], ALU.mult, ALU.mult),
                     [("yT", m), "gpost", "rr"], [("yT", m)])
                p.op("pool", TT(xg[:, m, :], xg[:, m, :], yT[:, m, :], ALU.add), ["xg", ("yT", m)], ["xg"])
            c0 = pp * TP + g * 512
            p.dma("sp", xoT_v[:, :, c0:c0 + 512], xg[:], ["xg"], ["xo"], "xout")
            if emit_h_next:
                norm_stats(lambda k: xg[:, k, :], 512, 1.0 / D, 5, ["xg"])
                for k in range(KC):
                    p.op("dve", STT(hn[:, k, :], xg[:, k, :], gnext[:, k:k + 1], rr[:], ALU.mult, ALU.mult),
                         ["xg", "gnext", "rr"], ["hn"])
                p.dma("sp", hnT.rearrange("(k p) t -> p k t", p=128)[:, :, c0:c0 + 512], hn[:], ["hn"], ["hno"], "hout")
    fin = ["xo"] + (["hno"] if emit_h_next else [])
    p.op("sp", lambda e: None, fin, [])
    p.emit()
    p.close()
    return nc


def build_diff(S, layer_idx=1):
    NGq = S // 512
    ND = 4 * NGq
    NTL = S // 128
    lam_init = 0.8 - 0.6 * math.exp(-0.3 * layer_idx)
    nc = bass.Bass("TRN2", target_bir_lowering=False)
    dt = nc.dram_tensor
    hT_d = dt("hT", [D, S], BF16, kind="ExternalInput").ap()
    wblk = dt("wblk", [2, 128, KC, 512], F32, kind="ExternalInput").ap()
    tb_d = dt("tbias", [128, 2, ND], F32, kind="ExternalInput").ap()
    qaug_d = dt("qaug", [4, 512], BF16, kind="ExternalInput").ap()
    kaug_d = dt("kaug", [2, 4, S], BF16, kind="ExternalInput").ap()
    mneg_d = dt("mneg", [128, 512], F32, kind="ExternalInput").ap()
    lam_d = dt("lamv", [128, 4, 64], F32, kind="ExternalInput").ap()
    gsub_d = dt("gsub", [128, 1], F32, kind="ExternalInput").ap()
    yo_d = dt("ygT", [256, S], BF16, kind="ExternalOutput").ap()
    hT_v = hT_d.rearrange("(k p) t -> p k t", p=128)

    p = Prog(nc)
    KT = [p.sbuf("KT%d" % c, [68, S], BF16) for c in range(2)]
    V = p.sbuf("V", [128, NTL, 128], BF16)
    hTg = [p.sbuf("hTg%d" % i, [128, KC, 512], BF16) for i in range(2)]
    QTc = [[p.sbuf("QT%d_%d" % (i, c), [68, 512], BF16) for c in range(2)] for i in range(2)]
    wb = [p.sbuf("wb%d" % i, [128, KC, 512], BF16) for i in range(2)]
    PT = [p.sbuf("PT%d" % i, [128, 1024], BF16) for i in range(2)]
    sbm = p.sbuf("sbm", [128, 1024], F32)
    accb = p.sbuf("accb", [128, 1024], F32)
    acc = [accb[:, c * 512:(c + 1) * 512] for c in range(2)]
    sg = p.sbuf("sg", [128, 512], F32)
    rc = [p.sbuf("rc%d" % c, [128, 512], F32) for c in range(2)]
    t0b = p.sbuf("t0b", [128, 512], F32)
    t1b = p.sbuf("t1b", [128, 512], F32)
    ob = p.sbuf("ob", [128, 512], F32)
    sqo = p.sbuf("sqo", [128, 512], BF16)
    rs = p.sbuf("rs", [128, 512], F32)
    yq = [p.sbuf("yq%d" % i, [128, 512], BF16) for i in range(2)]
    tbias = p.sbuf("tbias_s", [128, 2, ND], F32)
    mneg = p.sbuf("mneg_s", [128, 512], F32)
    lamv = p.sbuf("lamv_s", [128, 4, 64], F32)
    lp = p.sbuf("lp", [128, 2, 64], F32)
    ls = p.sbuf("ls", [128, 2], F32)
    nlam = p.sbuf("nlam", [128, 1], F32)
    gsub = p.sbuf("gsub_s", [128, 1], F32)
    ones_b = p.sbuf("ones_b", [128, 128], BF16)
    ones_f = p.sbuf("ones_f", [128, 128], F32)
    psS = [p.psum("psS%d" % i, [128, 1024], F32) for i in range(2)]
    psO = [p.psum("psO%d" % c, [128, 512], F32) for c in range(2)]
    psA = p.psum("psA", [128, 512], F32)
    psB = p.psum("psB", [128, 512], F32)

    p.op("dve", MSET(ones_b[:], 1.0), [], ["ones_b"])
    p.op("dve", MSET(ones_f[:], 1.0), [], ["ones_f"])
    p.dma("sp", tbias[:], tb_d, [], ["tbias"], "c0")
    p.dma("sp", mneg[:], mneg_d, [], ["mneg"], "c1")
    p.dma("sp", lamv[:], lam_d, [], ["lamv"], "c2")
    p.dma("sp", gsub[:], gsub_d, [], ["gsub"], "c3")
    for i in range(2):
        for c in range(2):
            p.dma("sp", QTc[i][c][64:68, :], qaug_d, [], [("QT", i, c)], "c4_%d_%d" % (i, c))
    for j in range(2):
        p.op("dve", TT(lp[:, j, :], lamv[:, 2 * j, :], lamv[:, 2 * j + 1, :], ALU.mult), ["lamv"], ["lp"])
        p.op("dve", lambda e, j=j: e.reduce_sum(ls[:, j:j + 1], lp[:, j, :], axis=mybir.AxisListType.X), ["lp"], ["ls"])
    p.op("act", ACT(ls[:], ls[:], AF.Exp), ["ls"], ["ls"])
    p.op("dve", TT(nlam[:], ls[:, 1:2], ls[:, 0:1], ALU.subtract), ["ls"], ["nlam"])
    p.op("dve", TS(nlam[:], nlam[:], float(-lam_init), ALU.add), ["nlam"], ["nlam"])
    p.op("dve", TS(gsub[:], gsub[:], float(1.0 - lam_init), ALU.mult), ["gsub"], ["gsub"])

    for hh in range(2):
        s = hh
        p.dma("pool", wb[s][:], wblk[hh], [], [("wb", s)], "w%d" % s)
        for c in range(2):
            p.dma("sp", KT[c][64:68, :], kaug_d[hh], [], [("KTaug", c)], "ka%d" % c)
        for g in range(NGq):
            gi = g % 2
            p.dma("sp", hTg[gi][:], hT_v[:, :, g * 512:(g + 1) * 512], [], [("hTg", gi)], "h%d" % gi)
            hk = [("hTg", gi), ("wb", s)]
            for c in range(2):
                for k in range(KC):
                    p.op("pe", MM(psA[:64, :], wb[s][:, k, c * 64:(c + 1) * 64], hTg[gi][:, k, :], k == 0, k == KC - 1), hk, ["psA"])
                p.op("act", ACT(QTc[gi][c][0:64, :], psA[:64, :], AF.Copy, scale=0.125), ["psA"], [("QT", gi, c)])
                for k in range(KC):
                    p.op("pe", MM(psB[:64, :], wb[s][:, k, 128 + c * 64:128 + (c + 1) * 64], hTg[gi][:, k, :], k == 0, k == KC - 1), hk, ["psB"])
                p.op("dve", CP(KT[c][0:64, g * 512:(g + 1) * 512], psB[:64, :]), ["psB"], [("KT", c, g)])
            for t in range(4):
                for k in range(KC):
                    p.op("pe", MM(psA[:, t * 128:(t + 1) * 128], hTg[gi][:, k, t * 128:(t + 1) * 128], wb[s][:, k, 256:384], k == 0, k == KC - 1), hk, ["psA"])
            p.op("dve", CP(V[:, 4 * g:4 * g + 4, :], psA[:].rearrange("p (t e) -> p t e", e=128)), ["psA"], [("V", g)])
            for k in range(KC):
                p.op("pe", MM(psB[:], wb[s][:, k, 384:512], hTg[gi][:, k, :], k == 0, k == KC - 1), hk, ["psB"])
            p.op("act", ACT(sg[:], psB[:], AF.Silu), ["psB"], ["sg"])
            nkb = 4 * g + 4
            for kb in range(nkb):
                j = kb - 4 * g
                q0 = 128 * j if j > 0 else 0
                N = 512 - q0
                si = kb % 2
                delta = 4 * g + 3 - kb
                bias = tbias[:, hh, delta:delta + 1]
                kg = kb // 4
                for c in range(2):
                    p.op("pe", MM(psS[si][:, c * 512 + q0:(c + 1) * 512], KT[c][0:68, kb * 128:(kb + 1) * 128], QTc[gi][c][0:68, q0:512], True, True),
                         [("KT", c, kg), ("KTaug", c), ("QT", gi, c)], [("psS", si, c)])
                if j < 0:
                    p.op("act", ACT(PT[si][:], psS[si][:], AF.Exp, bias=bias), [("psS", si, 0), ("psS", si, 1), "tbias"], [("PT", si, 0), ("PT", si, 1)])
                else:
                    for c in range(2):
                        p.op("dve", TT(sbm[:, c * 512 + q0:(c + 1) * 512], psS[si][:, c * 512 + q0:(c + 1) * 512], mneg[:, 0:N], ALU.add),
                             [("psS", si, c), "mneg"], [("sbm", c)])
                        p.op("act", ACT(PT[si][:, c * 512 + q0:(c + 1) * 512], sbm[:, c * 512 + q0:(c + 1) * 512], AF.Exp, bias=bias),
                             [("sbm", c), "tbias"], [("PT", si, c)])
                if kb == 0:
                    p.op("dve", CP(accb[:], PT[si][:]), [("PT", si, 0), ("PT", si, 1)], [("acc", 0), ("acc", 1)])
                elif j <= 0:
                    p.op("dve", TT(accb[:], accb[:], PT[si][:], ALU.add), [("PT", si, 0), ("PT", si, 1), ("acc", 0), ("acc", 1)],
                         [("acc", 0), ("acc", 1)])
                else:
                    for c in range(2):
                        p.op("dve", TT(accb[:, c * 512 + q0:(c + 1) * 512], accb[:, c * 512 + q0:(c + 1) * 512],
                                       PT[si][:, c * 512 + q0:(c + 1) * 512], ALU.add), [("PT", si, c), ("acc", c)], [("acc", c)])
                for c in range(2):
                    src = PT[si][:, c * 512 + q0:(c + 1) * 512]
                    p.op("pe", MM(psO[c][:, q0:512], V[:, kb, :], src, kb == 0, kb == nkb - 1), [("V", kg), ("PT", si, c)], [("psO", c)])
            for c in range(2):
                ps_ = psA if c == 0 else psB
                pk = "psA" if c == 0 else "psB"
                p.op("pe", MM(ps_[:], ones_f[:], acc[c], True, True), ["ones_f", ("acc", c)], [pk])
                p.op("dve", RECIP(rc[c][:], ps_[:]), [pk], [("rc", c)])
            p.op("dve", TT(t0b[:], psO[0][:], rc[0][:], ALU.mult), [("psO", 0), ("rc", 0)], ["t0b"])
            p.op("dve", TT(t1b[:], psO[1][:], rc[1][:], ALU.mult), [("psO", 1), ("rc", 1)], ["t1b"])
            p.op("dve", STT(ob[:], t1b[:], nlam[:, 0:1], t0b[:], ALU.mult, ALU.add), ["t0b", "t1b", "nlam"], ["ob"])
            p.op("pool", TT(sqo[:], ob[:], ob[:], ALU.mult), ["ob"], ["sqo"])
            p.op("pe", MM(psA[:], ones_b[:], sqo[:], True, True), ["ones_b", "sqo"], ["psA"])
            p.op("act", ACT(rs[:], psA[:], AF.Ln, bias=EPS, scale=1.0 / 128), ["psA"], ["rs"])
            p.op("act", ACT(rs[:], rs[:], AF.Exp, scale=-0.5), ["rs"], ["rs"])
            p.op("dve", STT(ob[:], ob[:], gsub[:, 0:1], rs[:], ALU.mult, ALU.mult), ["ob", "gsub", "rs"], ["ob"])
            yi = g % 2
            p.op("dve", TT(yq[yi][:], ob[:], sg[:], ALU.mult), ["ob", "sg"], [("yq", yi)])
            p.dma("sp", yo_d[hh * 128:(hh + 1) * 128, g * 512:(g + 1) * 512], yq[yi][:], [("yq", yi)], ["yo"], "yo%d" % yi)
    p.op("sp", lambda e: None, ["yo"], [])
    p.emit()
    p.close()
    return nc


_PROGS = {}
_DBG = {}


def _prog(key, fn):
    if key not in _PROGS:
        _PROGS[key] = fn()
    return _PROGS[key]


def _launch(nc, in_maps):
    res = run_bass_kernel_spmd(nc, in_maps, core_ids=list(range(NCORES)))
    return res.results


def _tok_common(inp, i, memT):
    f = lambda a: np.asarray(a, np.float32)
    return {
        "memT": memT,
        "wkv": arrange_blocks(f(inp["w_mem_kv_%d" % i]), memkv_colsets()),
        "wo": arrange_wout(f(inp["w_out_%d" % i])),
        "gpre": arrange_vec(inp["norm_pre_%d" % i], KC),
        "gpost": arrange_vec(inp["norm_post_%d" % i], KC),
        "gmem": arrange_vec(inp["norm_mem_%d" % i], KC),
    }


def _halo_split(xT_full, TC, HALO):
    S = xT_full.shape[1]
    outs = []
    for c in range(NCORES):
        own = xT_full[:, c * TC:(c + 1) * TC]
        if HALO == 0:
            outs.append(np.ascontiguousarray(own))
            continue
        if c == 0:
            h = np.zeros((D, HALO), np.float32)
        else:
            h = xT_full[:, c * TC - HALO:c * TC]
        outs.append(np.ascontiguousarray(np.concatenate([h, own], axis=1)))
    return outs


def run_model(inp, S, stop_after=None):
    f = lambda a: np.asarray(a, np.float32)
    TC = S // NCORES
    TPc = min(1024, TC)
    TPs = min(512, TC)
    x = f(inp["x"])[0]
    xT_full = np.ascontiguousarray(x.T)
    memT = np.ascontiguousarray(f(inp["mem"])[0].T)

    def conv_layer(i, xT_full, emit_next):
        nc = _prog(("conv", TC, TPc, emit_next), lambda: build_tok("conv", TC, TPc, emit_h_next=emit_next))
        com = _tok_common(inp, i, memT)
        com["wblk"] = arrange_blocks(f(inp["w_in_%d" % i]), conv_colsets())
        cwv = f(inp["conv_w_%d" % i])
        com["cw"] = np.ascontiguousarray(cwv.reshape(3, 16, 128).transpose(2, 1, 0))
        if emit_next:
            com["gnext"] = arrange_vec(inp["norm_pre_%d" % (i + 1)], KC)
        xs = _halo_split(xT_full, TC, 16)
        res = _launch(nc, [dict(com, xT=xs[c]) for c in range(NCORES)])
        xo = np.concatenate([r["xoT"] for r in res], axis=1)
        hn = np.concatenate([np.asarray(r["hnT"]) for r in res], axis=1) if emit_next else None
        return xo, hn

    x1T, h1T = conv_layer(0, xT_full, True)
    if stop_after == 0:
        return x1T, h1T

    i = 1
    nc = _prog(("diff", S), lambda: build_diff(S, 1))
    slopes = alibi_slopes(16).astype(np.float64)
    NGq = S // 512
    ND = 4 * NGq
    w1 = f(inp["w_in_1"])
    ql = np.arange(512)
    qaug = bf(np.stack([ql // 16, ql % 16, ql // 16, ql % 16]).astype(np.float32))
    mneg = np.zeros((128, 512), np.float32)
    mneg[:, :128] = np.where(np.arange(128)[None, :] >= np.arange(128)[:, None], 0.0, NEG)
    lamv = np.stack([f(inp["lambda_q1_1"]), f(inp["lambda_k1_1"]), f(inp["lambda_q2_1"]), f(inp["lambda_k2_1"])])
    lamv = np.ascontiguousarray(np.broadcast_to(lamv[None], (128, 4, 64)))
    gsub = np.ascontiguousarray(f(inp["subln_1"]).reshape(128, 1))
    h1T = np.ascontiguousarray(h1T)
    maps = []
    for c in range(NCORES):
        colsets = []
        tb = np.zeros((128, 2, ND), np.float32)
        kaug = np.zeros((2, 4, S), np.float32)
        for hh in range(2):
            h = 2 * c + hh
            cs = []
            for base in (0, 2048):
                for cm in range(2):
                    cs += list(range(base + h * 128 + cm * 64, base + h * 128 + cm * 64 + 64))
            cs += list(range(4096 + h * 128, 4096 + h * 128 + 128))
            cs += list(range(6400 + h * 128, 6400 + h * 128 + 128))
            colsets.append(cs)
            sl = float(slopes[h])
            dl = np.arange(ND)[None, :]
            tb[:, hh, :] = (sl * (128.0 * (3 - dl) + np.arange(128)[:, None])).astype(np.float32)
            s_hi = float(np.float32(sl).astype(ml_dtypes.bfloat16))
            s_lo = float(np.float32(sl - s_hi).astype(ml_dtypes.bfloat16))
            kaug[hh] = np.array([-16 * s_hi, -s_hi, -16 * s_lo, -s_lo], np.float32)[:, None]
        maps.append({"hT": h1T, "wblk": arrange_blocks(w1, colsets), "tbias": tb, "qaug": qaug,
                     "kaug": bf(kaug), "mneg": mneg, "lamv": lamv, "gsub": gsub})
    res = _launch(nc, maps)
    mixT_full = np.concatenate([np.asarray(r["ygT"]) for r in res], axis=0)
    if stop_after == "1a":
        return mixT_full

    nc = _prog(("diffpost", TC, TPc), lambda: build_tok("diffpost", TC, TPc))
    com = _tok_common(inp, 1, memT)
    com["wblk"] = arrange_blocks(w1, diffpost_colsets())
    res = _launch(nc, [dict(com, xT=np.ascontiguousarray(x1T[:, c * TC:(c + 1) * TC]),
                            mixT=np.ascontiguousarray(mixT_full[:, c * TC:(c + 1) * TC])) for c in range(NCORES)])
    x2T = np.concatenate([r["xoT"] for r in res], axis=1)
    _DBG["x2T"] = x2T
    if stop_after == 1:
        return x2T

    nc = _prog(("swa", TC, TPs), lambda: build_tok("swa", TC, TPs))
    com = _tok_common(inp, 2, memT)
    com["wblk"] = arrange_blocks(f(inp["w_in_2"]), swa_colsets())
    sidx = np.arange(128)[:, None]
    qidx = np.arange(128)[None, :]
    rel = np.concatenate([qidx - sidx + 128, qidx - sidx], axis=1).astype(np.float32)
    com["rel2"] = rel
    com["mneg2"] = np.where((rel >= 0) & (rel < 128), 0.0, NEG).astype(np.float32)
    sk = f(inp["sinks_2"])
    com["sinks"] = np.ascontiguousarray(sk.reshape(16, 2)[:, (np.arange(128) // 64)].T)
    xs = _halo_split(x2T, TC, 128)
    maps = []
    for c in range(NCORES):
        hv = np.full((128, 1), NEG if c == 0 else 0.0, np.float32)
        maps.append(dict(com, xT=xs[c], hvneg=hv))
    res = _launch(nc, maps)
    x3T = np.concatenate([r["xoT"] for r in res], axis=1)
    _DBG["x3T"] = x3T
    if stop_after == 2:
        return x3T

    x4T, _ = conv_layer(3, x3T, False)
    return x4T


_INPUT_NAMES = [
    "x", "mem", "positions",
    "norm_pre_0", "norm_post_0", "norm_mem_0", "w_in_0", "w_mem_kv_0", "conv_w_0", "w_out_0",
    "norm_pre_1", "norm_post_1", "norm_mem_1", "w_in_1", "w_mem_kv_1",
    "lambda_q1_1", "lambda_k1_1", "lambda_q2_1", "lambda_k2_1", "subln_1", "w_out_1",
    "norm_pre_2", "norm_post_2", "norm_mem_2", "w_in_2", "w_mem_kv_2", "sinks_2", "w_out_2",
    "norm_pre_3", "norm_post_3", "norm_mem_3", "w_in_3", "w_mem_kv_3", "conv_w_3", "w_out_3",
]


def kernel(**inputs):
    inputs = {n: inputs[n] for n in _INPUT_NAMES}
    S = inputs["x"].shape[1]
    outT = run_model(inputs, S)
    return np.ascontiguousarray(outT.T)[None].astype(np.float32)
```
